# Optimizing a Trainium2 kernel written in Bass

```python
import math
import jax, jax.numpy as jnp
from jax import lax
import numpy as np

D_MODEL = 2048
BATCH = 8
SEQ = 4096
DEPTH = 4

CTX_LEN = 256
GRID_W = 64
N_EVEN = (DEPTH + 1) // 2
N_ODD = DEPTH // 2
D_FF = 5632
N_MOD = 9
EPS = 1e-6
CONV_A_CH = D_MODEL // 2
CONV_B_CH = D_MODEL // 2
CONV_A_WIDTH = 31
CONV_B_WIDTH = 3
CONV_IN = 2 * CONV_A_CH + 3 * CONV_B_CH
CONV_OUT_IN = CONV_A_CH + CONV_B_CH
SSM_EXPAND = 2
D_INNER = SSM_EXPAND * D_MODEL
HEAD_DIM = 64
N_SSM_HEADS = D_INNER // HEAD_DIM
D_STATE = 128
N_GROUPS = 8
HEADS_PER_GROUP = N_SSM_HEADS // N_GROUPS
SSM_CONV_WIDTH = 4
CHUNK = 128
GN = N_GROUPS * D_STATE
SSM_CONV_DIM = D_INNER + 2 * GN
SSM_DIR_COLS = 2 * GN + N_SSM_HEADS
SSM_IN = 2 * D_INNER + 2 * SSM_DIR_COLS

kernel_name = "hybrid_conv_ssd_diffusion_trunk"


def rmsnorm(h, g):
    h32 = h.astype(jnp.float32)
    out = h32 * lax.rsqrt(jnp.mean(h32 * h32, axis=-1, keepdims=True) + EPS)
    return (out * g.astype(jnp.float32)).astype(h.dtype)


def layernorm(h, g, b):
    h32 = h.astype(jnp.float32)
    mu = jnp.mean(h32, axis=-1, keepdims=True)
    var = jnp.mean(jnp.square(h32 - mu), axis=-1, keepdims=True)
    out = (h32 - mu) * lax.rsqrt(var + EPS) * g.astype(jnp.float32) + b.astype(jnp.float32)
    return out.astype(h.dtype)


def modulate(h, g, shift, scale):
    return rmsnorm(h, g) * (1 + scale) + shift


def swiglu(h, w1, w3, w2):
    return (jax.nn.silu(h @ w1) * (h @ w3)) @ w2


def dwconv(u, w, pad):
    return lax.conv_general_dilated(
        u, w[:, None, :].astype(u.dtype), window_strides=(1,), padding=[pad],
        dimension_numbers=("NWC", "WIO", "NWC"), feature_group_count=u.shape[-1])


def to_col_major(h, rows):
    b, L, d = h.shape
    return h.reshape(b, rows, GRID_W, d).transpose(0, 2, 1, 3).reshape(b, L, d)


def from_col_major(h, rows):
    b, L, d = h.shape
    return h.reshape(b, GRID_W, rows, d).transpose(0, 2, 1, 3).reshape(b, L, d)


def conv_mixer_seq(h, in_w, dwa_w, dwa_b, ln_g, ln_b, dwb_w, out_w):
    u = h @ in_w
    a_val, a_gate, b_gate, c_gate, v = jnp.split(
        u, [CONV_A_CH, 2 * CONV_A_CH, 2 * CONV_A_CH + CONV_B_CH, 2 * CONV_A_CH + 2 * CONV_B_CH], axis=-1)
    a = a_val * jax.nn.sigmoid(a_gate)
    a = dwconv(a, dwa_w, (CONV_A_WIDTH // 2, CONV_A_WIDTH // 2)) + dwa_b
    a = jax.nn.silu(layernorm(a, ln_g, ln_b))
    bq = b_gate * dwconv(c_gate * v, dwb_w, (CONV_B_WIDTH // 2, CONV_B_WIDTH // 2))
    return jnp.concatenate([a, bq], axis=-1) @ out_w


def ssd_chunked(xs, dt, A, Bm, Cm, h0):
    bt, L = xs.shape[:2]
    nc = L // CHUNK
    xdt = xs.astype(jnp.float32) * dt[..., None]
    a = dt * A
    xdt = jnp.moveaxis(xdt.reshape(bt, nc, CHUNK, N_GROUPS, HEADS_PER_GROUP, HEAD_DIM), 1, 0)
    a = jnp.moveaxis(a.reshape(bt, nc, CHUNK, N_GROUPS, HEADS_PER_GROUP), 1, 0)
    Bc = jnp.moveaxis(Bm.astype(jnp.float32).reshape(bt, nc, CHUNK, N_GROUPS, D_STATE), 1, 0)
    Cc = jnp.moveaxis(Cm.astype(jnp.float32).reshape(bt, nc, CHUNK, N_GROUPS, D_STATE), 1, 0)
    causal = jnp.tril(jnp.ones((CHUNK, CHUNK), dtype=bool))

    def body(h, inp):
        xc, ac, bc, cc = inp
        cum = jnp.moveaxis(jnp.cumsum(ac, axis=1), 1, -1)
        seg = cum[..., :, None] - cum[..., None, :]
        Lm = jnp.exp(jnp.where(causal, seg, -jnp.inf))
        scores = jnp.einsum("btgn,bsgn->bgts", cc, bc)
        y = jnp.einsum("bgts,bghts,bsghp->btghp", scores, Lm, xc)
        y = y + jnp.einsum("btgn,bghpn,bght->btghp", cc, h, jnp.exp(cum))
        last = cum[..., -1:]
        h_new = h * jnp.exp(last)[..., None] + jnp.einsum(
            "bsgn,bghs,bsghp->bghpn", bc, jnp.exp(last - cum), xc)
        return h_new, y

    h_fin, ys = lax.scan(body, h0, (xdt, a, Bc, Cc))
    y = jnp.moveaxis(ys, 0, 1).reshape(bt, L, N_SSM_HEADS, HEAD_DIM)
    return y, h_fin


def ssd_direction(xbc_c, dt_c, xbc_l, dt_l, conv_w, conv_b, dt_bias, a_log, d_skip):
    A = -jnp.exp(a_log.astype(jnp.float32))

    def prep(xbc, dt_raw):
        bt, L, _ = xbc.shape
        xbc = jax.nn.silu(dwconv(xbc, conv_w, (SSM_CONV_WIDTH - 1, 0)) + conv_b)
        xs, Bm, Cm = jnp.split(xbc, [D_INNER, D_INNER + GN], axis=-1)
        xs = xs.reshape(bt, L, N_SSM_HEADS, HEAD_DIM)
        Bm = Bm.reshape(bt, L, N_GROUPS, D_STATE)
        Cm = Cm.reshape(bt, L, N_GROUPS, D_STATE)
        dt = jax.nn.softplus(dt_raw.astype(jnp.float32) + dt_bias.astype(jnp.float32))
        return xs, dt, Bm, Cm

    xs_c, dtc, B_c, C_c = prep(xbc_c, dt_c)
    xs_l, dtl, B_l, C_l = prep(xbc_l, dt_l)
    h0 = jnp.zeros((xs_c.shape[0], N_GROUPS, HEADS_PER_GROUP, HEAD_DIM, D_STATE), jnp.float32)
    y_c, h_c = ssd_chunked(xs_c, dtc, A, B_c, C_c, h0)
    y_l, _ = ssd_chunked(xs_l, dtl, A, B_l, C_l, h_c)
    skip = d_skip.astype(jnp.float32)[:, None]
    return (y_c + skip * xs_c.astype(jnp.float32),
            y_l + skip * xs_l.astype(jnp.float32))


def ssm_mixer(h_l, h_c, in_w, conv_w, conv_b, dt_bias, a_log, d_skip, norm_g, out_w):
    u_l = h_l @ in_w
    u_c = h_c @ in_w
    split_pts = [D_INNER, 2 * D_INNER, 2 * D_INNER + SSM_DIR_COLS]
    z_l, xr_l, f_l, b_l = jnp.split(u_l, split_pts, axis=-1)
    z_c, xr_c, f_c, b_c = jnp.split(u_c, split_pts, axis=-1)

    def dir_inputs(xr, cols):
        bc, dt = jnp.split(cols, [2 * GN], axis=-1)
        return jnp.concatenate([xr, bc], axis=-1), dt

    flip = lambda t: jnp.flip(t, axis=1)
    xcf, dcf = dir_inputs(xr_c, f_c)
    xlf, dlf = dir_inputs(xr_l, f_l)
    xcb, dcb = dir_inputs(xr_c, b_c)
    xlb, dlb = dir_inputs(xr_l, b_l)
    yf_c, yf_l = ssd_direction(xcf, dcf, xlf, dlf,
                               conv_w[0], conv_b[0], dt_bias[0], a_log[0], d_skip[0])
    yb_c, yb_l = ssd_direction(flip(xcb), flip(dcb), flip(xlb), flip(dlb),
                               conv_w[1], conv_b[1], dt_bias[1], a_log[1], d_skip[1])
    y_c = yf_c + flip(yb_c)
    y_l = yf_l + flip(yb_l)

    def out(y, z):
        bt, L = y.shape[:2]
        y = y.reshape(bt, L, D_INNER) * jax.nn.silu(z.astype(jnp.float32))
        y = y.reshape(bt, L, N_GROUPS, D_INNER // N_GROUPS)
        y = y * lax.rsqrt(jnp.mean(y * y, axis=-1, keepdims=True) + EPS)
        y = y.reshape(bt, L, D_INNER) * norm_g.astype(jnp.float32)
        return y.astype(z.dtype) @ out_w

    return out(y_l, z_l), out(y_c, z_c)


def setup_inputs(seed: int = 0) -> dict:
    key = jax.random.key(seed)
    ks = jax.random.split(key, 32)
    f32 = jnp.float32

    def nrm(k, shape, scale):
        return jax.random.normal(k, shape, f32) * scale

    dt0 = jnp.exp(jax.random.uniform(ks[24], (N_ODD, 2, N_SSM_HEADS), f32,
                                     minval=math.log(1e-3), maxval=math.log(1e-1)))
    return {
        "x": nrm(ks[0], (BATCH, SEQ, D_MODEL), 1.0),
        "c": nrm(ks[1], (BATCH, D_MODEL), 1.0),
        "ctx": nrm(ks[2], (BATCH, CTX_LEN, D_MODEL), 1.0),
        "c_ctx": nrm(ks[3], (D_MODEL,), 1.0),
        "mod_w": nrm(ks[4], (DEPTH, D_MODEL, N_MOD * D_MODEL), 0.5 * D_MODEL ** -0.5),
        "mod_b": nrm(ks[5], (DEPTH, N_MOD * D_MODEL), 0.01),
        "norm_g": 1.0 + nrm(ks[6], (DEPTH, 3, D_MODEL), 0.02),
        "ffn_w1": nrm(ks[7], (DEPTH, 2, D_MODEL, D_FF), D_MODEL ** -0.5),
        "ffn_w3": nrm(ks[8], (DEPTH, 2, D_MODEL, D_FF), D_MODEL ** -0.5),
        "ffn_w2": nrm(ks[9], (DEPTH, 2, D_FF, D_MODEL), D_FF ** -0.5),
        "conv_in_w": nrm(ks[10], (N_EVEN, D_MODEL, CONV_IN), D_MODEL ** -0.5),
        "conv_a_w": nrm(ks[11], (N_EVEN, CONV_A_WIDTH, CONV_A_CH), CONV_A_WIDTH ** -0.5),
        "conv_a_b": nrm(ks[12], (N_EVEN, CONV_A_CH), 0.01),
        "conv_ln_g": 1.0 + nrm(ks[13], (N_EVEN, CONV_A_CH), 0.02),
        "conv_ln_b": nrm(ks[14], (N_EVEN, CONV_A_CH), 0.01),
        "conv_b_w": nrm(ks[15], (N_EVEN, CONV_B_WIDTH, CONV_B_CH), CONV_B_WIDTH ** -0.5),
        "conv_out_w": nrm(ks[16], (N_EVEN, CONV_OUT_IN, D_MODEL), CONV_OUT_IN ** -0.5),
        "ssm_in_w": nrm(ks[17], (N_ODD, D_MODEL, SSM_IN), D_MODEL ** -0.5),
        "ssm_conv_w": nrm(ks[18], (N_ODD, 2, SSM_CONV_WIDTH, SSM_CONV_DIM), SSM_CONV_WIDTH ** -0.5),
        "ssm_conv_b": nrm(ks[19], (N_ODD, 2, SSM_CONV_DIM), 0.01),
        "ssm_dt_bias": dt0 + jnp.log(-jnp.expm1(-dt0)),
        "ssm_a_log": jnp.log(jax.random.uniform(ks[20], (N_ODD, 2, N_SSM_HEADS), f32, minval=1.0, maxval=16.0)),
        "ssm_d": 1.0 + nrm(ks[21], (N_ODD, 2, N_SSM_HEADS), 0.1),
        "ssm_norm_g": 1.0 + nrm(ks[22], (N_ODD, D_INNER), 0.02),
        "ssm_out_w": nrm(ks[23], (N_ODD, D_INNER, D_MODEL), D_INNER ** -0.5),
        "final_g": 1.0 + nrm(ks[25], (D_MODEL,), 0.02),
    }


def reference(x, c, ctx, c_ctx, mod_w, mod_b, norm_g, ffn_w1, ffn_w3, ffn_w2,
              conv_in_w, conv_a_w, conv_a_b, conv_ln_g, conv_ln_b, conv_b_w, conv_out_w,
              ssm_in_w, ssm_conv_w, ssm_conv_b, ssm_dt_bias, ssm_a_log, ssm_d, ssm_norm_g,
              ssm_out_w, final_g):
    bt = x.shape[0]
    rows = x.shape[1] // GRID_W
    h_l, h_c = x, ctx
    silu_c = jax.nn.silu(c)
    silu_cc = jax.nn.silu(c_ctx)
    for i in range(DEPTH):
        last = i == DEPTH - 1
        mod_l = (silu_c @ mod_w[i] + mod_b[i]).reshape(bt, N_MOD, 1, D_MODEL)
        mod_c = (silu_cc @ mod_w[i] + mod_b[i]).reshape(N_MOD, 1, 1, D_MODEL)
        ml = [mod_l[:, k] for k in range(N_MOD)]
        mc = [mod_c[k] for k in range(N_MOD)]

        w1a, w3a, w2a = ffn_w1[i, 0], ffn_w3[i, 0], ffn_w2[i, 0]
        h_l = h_l + 0.5 * ml[2] * swiglu(modulate(h_l, norm_g[i, 0], ml[0], ml[1]), w1a, w3a, w2a)
        h_c = h_c + 0.5 * mc[2] * swiglu(modulate(h_c, norm_g[i, 0], mc[0], mc[1]), w1a, w3a, w2a)

        n_l = modulate(h_l, norm_g[i, 1], ml[3], ml[4])
        n_c = modulate(h_c, norm_g[i, 1], mc[3], mc[4])
        if i % 2 == 0:
            e = i // 2
            cp = (conv_in_w[e], conv_a_w[e], conv_a_b[e], conv_ln_g[e], conv_ln_b[e],
                  conv_b_w[e], conv_out_w[e])
            h_l = h_l + ml[5] * conv_mixer_seq(n_l, *cp)
            if not last:
                h_c = h_c + mc[5] * conv_mixer_seq(n_c, *cp)
        else:
            o = i // 2
            col_major = (o % 2 == 1)
            if col_major:
                n_l = to_col_major(n_l, rows)
            y_l, y_c = ssm_mixer(n_l, n_c, ssm_in_w[o], ssm_conv_w[o], ssm_conv_b[o],
                                 ssm_dt_bias[o], ssm_a_log[o], ssm_d[o], ssm_norm_g[o], ssm_out_w[o])
            if col_major:
                y_l = from_col_major(y_l, rows)
            h_l = h_l + ml[5] * y_l
            if not last:
                h_c = h_c + mc[5] * y_c

        w1b, w3b, w2b = ffn_w1[i, 1], ffn_w3[i, 1], ffn_w2[i, 1]
        h_l = h_l + 0.5 * ml[8] * swiglu(modulate(h_l, norm_g[i, 2], ml[6], ml[7]), w1b, w3b, w2b)
        if not last:
            h_c = h_c + 0.5 * mc[8] * swiglu(modulate(h_c, norm_g[i, 2], mc[6], mc[7]), w1b, w3b, w2b)
    return rmsnorm(h_l, final_g)
```

```python
import numpy as np
from contextlib import ExitStack
import concourse.bass as bass
import concourse.mybir as mybir
from concourse.bass_utils import run_bass_kernel_spmd

F32 = mybir.dt.float32
BF16 = mybir.dt.bfloat16
AF = mybir.ActivationFunctionType
ALU = mybir.AluOpType

D = 2048
NCH = 16
DFF = 5632
NFF = 44
LCTX = 256
LLAT = 4096
T = LCTX + LLAT
TILES = [(0, 256)] + [(256 + 512 * i, 512) for i in range(8)]
EPS = 1e-6
NMOD = 9


class Buf:
    __slots__ = ("name", "w", "r")

    def __init__(self, name):
        self.name = name
        self.w = {}
        self.r = {}


class Sched:
    def __init__(self, nc, es):
        self.nc = nc
        self.es = es
        self.E = {"pe": nc.tensor, "act": nc.scalar, "dve": nc.vector, "pool": nc.gpsimd, "sp": nc.sync}
        self.semh = {}
        self.cnt = {}
        self.seen = {e: {} for e in self.E}
        self.bufs = []
        for e in ("pe", "act", "dve", "pool"):
            self._newsem("E_" + e)

    def buf(self, name):
        b = Buf(name)
        self.bufs.append(b)
        return b

    def _newsem(self, key):
        self.semh[key] = self.es.enter_context(self.nc.semaphore(key))
        self.cnt[key] = 0

    def _wait(self, e, need, is_dma=False):
        eng = self.E[e]
        own = "E_" + e
        for k, v in need.items():
            if v <= 0:
                continue
            if k == own and (e == "pe" or (e == "dve" and not is_dma)):
                continue
            if self.seen[e].get(k, 0) >= v:
                continue
            eng.wait_ge(self.semh[k], v)
            self.seen[e][k] = v

    @staticmethod
    def _deps(reads, writes):
        need = {}
        for b in reads:
            for k, v in b.w.items():
                if need.get(k, 0) < v:
                    need[k] = v
        for b in writes:
            for dct in (b.w, b.r):
                for k, v in dct.items():
                    if need.get(k, 0) < v:
                        need[k] = v
        return need

    def op(self, e, fn, reads=(), writes=(), inc=True):
        self._wait(e, self._deps(reads, writes))
        ins = fn(self.E[e])
        k = "E_" + e
        if inc:
            self.cnt[k] += 1
            ins.then_inc(self.semh[k], 1)
            v = self.cnt[k]
        else:
            v = self.cnt[k] + 1
        for b in reads:
            if b.r.get(k, 0) < v:
                b.r[k] = v
        for b in writes:
            b.w[k] = v
            b.r = {}
        return ins

    def dma(self, q, out, in_, key, reads=(), writes=(), **kw):
        key = "D_" + key
        need = self._deps(reads, writes)
        need.pop(key, None)
        self._wait(q, need, is_dma=True)
        if key not in self.semh:
            self._newsem(key)
        ins = self.E[q].dma_start(out=out, in_=in_, **kw)
        self.cnt[key] += 16
        ins.then_inc(self.semh[key], 16)
        v = self.cnt[key]
        for b in reads:
            if b.r.get(key, 0) < v:
                b.r[key] = v
        for b in writes:
            b.w[key] = v
            b.r = {}
        return ins

    def barrier(self, final=False):
        full = dict(self.cnt) if final else {k: v for k, v in self.cnt.items() if not k.startswith("D_cvt")}
        for e in self.E:
            self._wait(e, full)
        for b in self.bufs:
            if b.name.startswith("G_wb"):
                continue
            b.w = {}
            b.r = {}
        self.bufs = [b for b in self.bufs if b.name.startswith("G_")]


class Prog:
    def __init__(self, cfg):
        self.cfg = cfg
        self.nc = bass.Bass("TRN2", target_bir_lowering=False)
        self.es = ExitStack()
        self.S = None
        self.dram = {}
        self.uid = 0

    def din(self, name, shape, dt=F32):
        t = self.nc.dram_tensor(name, list(shape), dt, kind="ExternalInput").ap()
        self.dram[name] = t
        return t

    def dout(self, name, shape, dt=F32):
        t = self.nc.dram_tensor(name, list(shape), dt, kind="ExternalOutput").ap()
        self.dram[name] = t
        return t

    def dscr(self, name, shape, dt=F32):
        kind = "ExternalOutput" if name in self.cfg.get("debug_out", ()) else "Internal"
        t = self.nc.dram_tensor(name, list(shape), dt, kind=kind).ap()
        self.dram[name] = t
        return t

    def sb(self, es, name, shape, dt):
        self.uid += 1
        return es.enter_context(self.nc.sbuf_tensor(f"{name}_{self.uid}", list(shape), dt))

    def ps(self, es, name, shape, dt=F32):
        self.uid += 1
        return es.enter_context(self.nc.psum_tensor(f"{name}_{self.uid}", list(shape), dt))


def build(cfg):
    P = Prog(cfg)
    nc = P.nc
    with P.es as es:
        S = Sched(nc, es)
        P.S = S
        xT = P.din("xT", [D, T])
        sc_in = P.din("sc_in", [128, NCH, 2])
        mod_w = P.din("mod_w", [4, D, NMOD * D])
        mod_bT = P.din("mod_bT", [128, 4, NMOD * NCH])
        ngT = P.din("ngT", [128, 4, 3, NCH])
        fgT = P.din("fgT", [128, NCH])
        ffn_w1 = P.din("ffn_w1", [4, 2, D, DFF])
        ffn_w3 = P.din("ffn_w3", [4, 2, D, DFF])
        ffn_w2 = P.din("ffn_w2", [4, 2, DFF, D])
        consts = P.din("consts", [128, 768])
        conv_in_w = P.din("conv_in_w", [2, D, 5120])
        conv_out_w = P.din("conv_out_w", [2, D, D])
        convp = P.din("convp", [128, 2, 8, 37])
        ssm_in_w = P.din("ssm_in_w", [2, D, 12416])
        ssm_out_w = P.din("ssm_out_w", [2, 4096, D])
        scw = P.din("scw", [128, 2, 2, 48, 5])
        hp = P.din("hp", [64, 2, 2, 2])
        dsk = P.din("dsk", [128, 2, 2, 32])
        sng = P.din("sng", [128, 2, 32])
        selc = P.din("selc", [64, 64 * 128])
        P.wbi = P.dscr("wbi", [2, 49, 128, NCH * 256], BF16)
        P.wbo = P.dscr("wbo", [2, 8, 128, 32 * 256], BF16)
        P.ssm_src = (ssm_in_w, ssm_out_w)
        P.wbuf_ssm = {o: S.buf(f"G_wbs{o}") for o in range(2)}
        P.U = P.dscr("U", [12416, T])
        P.XS = P.dscr("XS", [2, 4096, T])
        P.BC = P.dscr("BC", [2, 2048, T])
        P.DTS = P.dscr("DTS", [2, 2, 64, T])
        P.YS = P.dscr("YS", [2, 4096, T])
        P.YSUM = P.dscr("YSUM", [4096, T])
        P.consts = consts
        P.apre = P.dscr("apre", [1024, T])
        P.cvs = P.dscr("cvs", [1024, T])
        P.bgs = P.dscr("bgs", [1024, T])
        outT = P.dout("outT", [D, LLAT])
        hT = P.dscr("hT", [D, T])
        P.wb1 = P.dscr("wb1", [4, 2, 22, 128, NCH * 256], BF16)
        P.wb3 = P.dscr("wb3", [4, 2, 22, 128, NCH * 256], BF16)
        P.wb2 = P.dscr("wb2", [4, 2, 8, 128, NFF * 256], BF16)
        P.ffn_src = (ffn_w1, ffn_w3, ffn_w2)
        P.ffn_list = [(l, 0 if sub == "ffn0" else 1) for (l, sub) in cfg["steps"] if sub in ("ffn0", "ffn1")]
        P.wbuf = {ls: S.buf(f"G_wb{ls[0]}{ls[1]}") for ls in P.ffn_list}
        P.cvt_n = 0
        P.hT = hT

        ones_f = P.sb(es, "ones_f", [128, 128], F32)
        modv = P.sb(es, "modv", [128, 2, NMOD * NCH], F32)
        av = P.sb(es, "av", [128, 2, 3, NCH], F32)
        gv = P.sb(es, "gv", [128, 2, 3, NCH], F32)
        ngs = P.sb(es, "ngs", [128, 4, 3, NCH], F32)
        fgs = P.sb(es, "fgs", [128, NCH], F32)
        mbs = P.sb(es, "mbs", [128, 4, NMOD * NCH], F32)
        scs = P.sb(es, "scs", [128, NCH, 2], F32)
        scb = P.sb(es, "scb", [128, NCH, 2], BF16)
        G = {k: S.buf("G_" + k) for k in ("ones", "modv", "av", "gv", "ngs", "fgs", "mbs", "scs", "scb")}
        P.g = dict(ones_f=ones_f, modv=modv, av=av, gv=gv, ngs=ngs, fgs=fgs, mbs=mbs, scs=scs, scb=scb, G=G)

        S.op("dve", lambda e: e.memset(ones_f[:], 1.0), writes=[G["ones"]])
        S.dma("sp", ngs[:], ngT[:, :, :, :], "c0", writes=[G["ngs"]])
        S.dma("sp", fgs[:], fgT[:, :], "c1", writes=[G["fgs"]])
        S.dma("sp", mbs[:], mod_bT[:, :, :], "c2", writes=[G["mbs"]])
        S.dma("sp", scs[:], sc_in[:, :, :], "c3", writes=[G["scs"]])
        S.op("act", lambda e: e.activation(scs[:], scs[:], AF.Silu), reads=[G["scs"]], writes=[G["scs"]])
        S.op("act", lambda e: e.copy(scb[:], scs[:]), reads=[G["scs"]], writes=[G["scb"]])
        S.barrier()

        if P.ffn_list:
            cvt_ffn(P, P.ffn_list[0])
        phase_copy_in(P, xT, hT)

        for (layer, sub) in cfg["steps"]:
            if sub == "mod":
                if layer % 2 == 1 and (layer, "mix") in cfg["steps"]:
                    cvt_ssm(P, layer // 2)
                phase_mod(P, layer, mod_w)
            elif sub in ("ffn0", "ffn1"):
                s = 0 if sub == "ffn0" else 1
                (phase_ffn2 if P.cfg.get('ffn2', True) else phase_ffn)(P, layer, s, ffn_w1, ffn_w3, ffn_w2)
            elif sub == "mix" and layer % 2 == 0:
                phase_conv1(P, layer, conv_in_w)
                phase_conv3(P, layer, conv_out_w, convp)
            elif sub == "mix" and layer % 2 == 1:
                phase_ssm1(P, layer, ssm_in_w)
                phase_ssm2(P, layer, scw, hp)
                phase_ssm3(P, layer, dsk, selc)
                phase_ssm3b(P, layer)
                phase_ssm4(P, layer, ssm_out_w, sng)
        phase_final(P, outT, cfg.get("final_norm", True))
        S.barrier(final=True)
    return P


def phase_copy_in(P, xT, hT):
    S, nc = P.S, P.nc
    with ExitStack() as es:
        bufs = [P.sb(es, f"cpi{i}", [128, T], F32) for i in range(2)]
        bb = [S.buf(f"cpi{i}") for i in range(2)]
        for c in range(NCH):
            i = c % 2
            S.dma("sp", bufs[i][:], xT[c * 128:(c + 1) * 128, :], f"cpi{i}l", writes=[bb[i]])
            S.dma("act", hT[c * 128:(c + 1) * 128, :], bufs[i][:], f"cpi{i}s", reads=[bb[i]])
        S.barrier()


def phase_mod(P, layer, mod_w):
    S, nc, g = P.S, P.nc, P.g
    G = g["G"]
    NJ = NMOD * NCH
    SL = 512
    with ExitStack() as es:
        wb = [P.sb(es, f"mw{i}", [128, NCH, SL], BF16) for i in range(3)]
        wbb = [S.buf(f"mw{i}") for i in range(3)]
        pm = P.ps(es, "pm", [128, NJ, 2], F32)
        pmb = S.buf("pm")
        nsl = NMOD * D // SL
        for s in range(nsl):
            i = s % 3
            src = mod_w[layer, :, s * SL:(s + 1) * SL].rearrange("(kc p) n -> p kc n", p=128)
            S.dma("pool", wb[i][:], src, f"mw{i}", writes=[wbb[i]])
            for jj in range(SL // 128):
                j = s * (SL // 128) + jj
                for kc in range(NCH):
                    S.op("pe", lambda e, i=i, jj=jj, kc=kc, j=j: e.matmul(
                        pm[:, j, :], wb[i][:, kc, jj * 128:(jj + 1) * 128], g["scb"][:, kc, :],
                        start=(kc == 0), stop=(kc == NCH - 1)),
                        reads=[wbb[i], G["scb"]], writes=[pmb], inc=(kc == NCH - 1))
        for s in range(2):
            S.op("dve", lambda e, s=s: e.tensor_tensor(g["modv"][:, s, :], pm[:, :, s], g["mbs"][:, layer, :], ALU.add),
                 reads=[pmb, G["mbs"]], writes=[G["modv"]])
        for s in range(2):
            for n in range(3):
                sc = g["modv"][:, s, (3 * n + 1) * NCH:(3 * n + 2) * NCH]
                S.op("dve", lambda e, s=s, n=n, sc=sc: e.scalar_tensor_tensor(
                    g["av"][:, s, n, :], sc, 1.0, g["ngs"][:, layer, n, :], ALU.add, ALU.mult),
                    reads=[G["modv"], G["ngs"]], writes=[G["av"]])
                gt = g["modv"][:, s, (3 * n + 2) * NCH:(3 * n + 3) * NCH]
                S.op("dve", lambda e, s=s, n=n, gt=gt: e.tensor_scalar_mul(
                    g["gv"][:, s, n, :], gt, 1.0 if n == 1 else 0.5),
                    reads=[G["modv"]], writes=[G["gv"]])
        S.barrier()


def norm_tile(P, es_bufs, ht, htb, xm, xmb, w, s, n, psb, psbb, tmp, tmpb, chunks=NCH, eps=EPS, dim=D):
    S, g = P.S, P.g
    G = g["G"]
    for c in range(chunks):
        S.op("act", lambda e, c=c: e.activation(tmp[:, c % 2, :w], ht[:, c, :w], AF.Square),
             reads=[htb], writes=[tmpb[c % 2]])
        S.op("pe", lambda e, c=c: e.matmul(psb[:, :w], g["ones_f"][:], tmp[:, c % 2, :w],
                                           start=(c == 0), stop=(c == chunks - 1)),
             reads=[tmpb[c % 2], G["ones"]], writes=[psbb], inc=True)
    S.op("act", lambda e: e.activation(tmp[:, 2, :w], psb[:, :w], AF.Sqrt, bias=eps, scale=1.0 / dim),
         reads=[psbb], writes=[tmpb[2]])
    S.op("dve", lambda e: e.reciprocal(tmp[:, 2, :w], tmp[:, 2, :w]),
         reads=[tmpb[2]], writes=[tmpb[2]])
    for c in range(chunks):
        S.op("dve", lambda e, c=c: e.tensor_tensor(tmp[:, c % 2, :w], ht[:, c, :w], tmp[:, 2, :w], ALU.mult),
             reads=[htb, tmpb[2]], writes=[tmpb[c % 2]])
        S.op("act", lambda e, c=c: e.activation(xm[:, c, :w], tmp[:, c % 2, :w], AF.Identity,
                                                bias=g["modv"][:, s, 3 * n * NCH + c:3 * n * NCH + c + 1],
                                                scale=g["av"][:, s, n, c:c + 1]),
             reads=[tmpb[c % 2], G["modv"], G["av"]], writes=[xmb])


def phase_ffn(P, layer, sidx, ffn_w1, ffn_w3, ffn_w2):
    S, nc, g = P.S, P.nc, P.g
    G = g["G"]
    hT = P.hT
    n = 0 if sidx == 0 else 2
    last_ctx_skip = (layer == 3 and sidx == 1)
    FS = 256
    CS = 256
    with ExitStack() as es:
        ht = P.sb(es, "ht", [128, NCH, 512], F32)
        xm = P.sb(es, "xm", [128, NCH, 512], BF16)
        hid = P.sb(es, "hid", [128, NFF, 512], BF16)
        tmp = P.sb(es, "tmp", [128, 3, 512], F32)
        NW = 3
        wsl = [P.sb(es, f"wsl{i}", [128, NFF * CS], BF16) for i in range(NW)]
        htb, xmb, tmpb = S.buf("ht"), S.buf("xm"), [S.buf(f"tmp{i}") for i in range(3)]
        hidb = [S.buf(f"hid{f}") for f in range(NFF)]
        wslb = [S.buf(f"wsl{i}") for i in range(NW)]
        pss = [P.ps(es, f"ps{i}", [128, 512], F32) for i in range(7)]
        pssb = [S.buf(f"ps{i}") for i in range(7)]
        wi = 0
        for ti, (t0, w) in enumerate(TILES):
            s = 1 if ti == 0 else 0
            if ti == 0 and last_ctx_skip:
                continue
            S.dma("sp", ht[:, :, :w], hT[:, t0:t0 + w].rearrange("(c p) t -> p c t", p=128), "ht_l", writes=[htb])
            norm_tile(P, None, ht, htb, xm, xmb, w, s, n, pss[6], pssb[6], tmp, tmpb)
            for sl in range(DFF // FS):
                i = wi % NW
                wi += 1
                wt = wsl[i]
                w1v = wt[:, 0:NCH * FS].rearrange("p (k f) -> p k f", k=NCH)
                w3v = wt[:, NCH * FS:2 * NCH * FS].rearrange("p (k f) -> p k f", k=NCH)
                S.dma("pool", w1v, ffn_w1[layer, sidx, :, sl * FS:(sl + 1) * FS].rearrange("(k p) f -> p k f", p=128),
                      f"wsl{i}", writes=[wslb[i]])
                S.dma("pool", w3v, ffn_w3[layer, sidx, :, sl * FS:(sl + 1) * FS].rearrange("(k p) f -> p k f", p=128),
                      f"wsl{i}", writes=[wslb[i]])
                for ff in range(FS // 128):
                    f = sl * (FS // 128) + ff
                    p1, p3 = pss[(2 * f) % 6], pss[(2 * f + 1) % 6]
                    p1b, p3b = pssb[(2 * f) % 6], pssb[(2 * f + 1) % 6]
                    for k in range(NCH):
                        S.op("pe", lambda e, k=k, ff=ff, p1=p1, w1v=w1v: e.matmul(
                            p1[:, :w], w1v[:, k, ff * 128:(ff + 1) * 128], xm[:, k, :w],
                            start=(k == 0), stop=(k == NCH - 1)),
                            reads=[wslb[i], xmb], writes=[p1b], inc=(k == NCH - 1))
                    for k in range(NCH):
                        S.op("pe", lambda e, k=k, ff=ff, p3=p3, w3v=w3v: e.matmul(
                            p3[:, :w], w3v[:, k, ff * 128:(ff + 1) * 128], xm[:, k, :w],
                            start=(k == 0), stop=(k == NCH - 1)),
                            reads=[wslb[i], xmb], writes=[p3b], inc=(k == NCH - 1))
                    tb = tmpb[f % 2]
                    S.op("act", lambda e, p1=p1, f=f: e.activation(tmp[:, f % 2, :w], p1[:, :w], AF.Silu),
                         reads=[p1b], writes=[tb])
                    S.op("dve", lambda e, p3=p3, f=f: e.tensor_tensor(hid[:, f, :w], tmp[:, f % 2, :w], p3[:, :w], ALU.mult),
                         reads=[tb, p3b], writes=[hidb[f]])
            for sl in range(D // CS):
                i = wi % NW
                wi += 1
                wt = wsl[i]
                w2v = wt[:, :].rearrange("p (f c) -> p f c", f=NFF)
                S.dma("pool", w2v, ffn_w2[layer, sidx, :, sl * CS:(sl + 1) * CS].rearrange("(f p) c -> p f c", p=128),
                      f"wsl{i}", writes=[wslb[i]])
                for cc in range(CS // 128):
                    c = sl * (CS // 128) + cc
                    pp, ppb = pss[c % 6], pssb[c % 6]
                    for f in range(NFF):
                        S.op("pe", lambda e, f=f, cc=cc, pp=pp, w2v=w2v: e.matmul(
                            pp[:, :w], w2v[:, f, cc * 128:(cc + 1) * 128], hid[:, f, :w],
                            start=(f == 0), stop=(f == NFF - 1)),
                            reads=[wslb[i], hidb[f]], writes=[ppb], inc=(f == NFF - 1))
                    S.op("dve", lambda e, c=c, pp=pp: e.scalar_tensor_tensor(
                        ht[:, c, :w], pp[:, :w], g["gv"][:, s, n, c:c + 1], ht[:, c, :w], ALU.mult, ALU.add),
                        reads=[ppb, G["gv"], htb], writes=[htb])
            S.dma("act", hT[:, t0:t0 + w].rearrange("(c p) t -> p c t", p=128), ht[:, :, :w], "ht_s", reads=[htb])
        S.barrier()


def cvt_ssm(P, o_):
    S = P.S
    w_in, w_out = P.ssm_src
    key = "cvts"
    wbuf = P.wbuf_ssm[o_]
    NCOL = 12416
    for sl in range(49):
        c0 = sl * 256
        cw = min(256, NCOL - c0)
        S.dma("pool",
              P.wbi[o_, sl, :, :].rearrange("p (k f) -> p k f", k=NCH)[:, :, :cw],
              w_in[o_, :, c0:c0 + cw].rearrange("(k p) f -> p k f", p=128),
              key, writes=[wbuf])
    for sl in range(8):
        S.dma("pool",
              P.wbo[o_, sl, :, :].rearrange("p (k f) -> p k f", k=32),
              w_out[o_, :, sl * 256:(sl + 1) * 256].rearrange("(k p) f -> p k f", p=128),
              key, writes=[wbuf])


def cvt_ffn(P, ls):
    S = P.S
    layer, sidx = ls
    w1, w3, w2 = P.ffn_src
    key = f"cvt{P.cvt_n % 2}"
    P.cvt_n += 1
    wbuf = P.wbuf[ls]
    for (src, dst) in ((w1, P.wb1), (w3, P.wb3)):
        for sl in range(22):
            S.dma("pool",
                  dst[layer, sidx, sl, :, :].rearrange("p (k f) -> p k f", k=NCH),
                  src[layer, sidx, :, sl * 256:(sl + 1) * 256].rearrange("(k p) f -> p k f", p=128),
                  key, writes=[wbuf])
    for sl in range(8):
        S.dma("pool",
              P.wb2[layer, sidx, sl, :, :].rearrange("p (f c) -> p f c", f=NFF),
              w2[layer, sidx, :, sl * 256:(sl + 1) * 256].rearrange("(f p) c -> p f c", p=128),
              key, writes=[wbuf])


def norm_steps(P, ht, htb, xm, xmb, w, s, n, psb, psbb, tmp, tmpb, load):
    S, g = P.S, P.g
    G = g["G"]
    load()
    yield
    for c in range(NCH):
        S.op("act", lambda e, c=c: e.activation(tmp[:, c % 2, :w], ht[:, c, :w], AF.Square),
             reads=[htb], writes=[tmpb[c % 2]])
        S.op("pe", lambda e, c=c: e.matmul(psb[:, :w], g["ones_f"][:], tmp[:, c % 2, :w],
                                           start=(c == 0), stop=(c == NCH - 1)),
             reads=[tmpb[c % 2], G["ones"]], writes=[psbb], inc=True)
        yield
    S.op("act", lambda e: e.activation(tmp[:, 2, :w], psb[:, :w], AF.Sqrt, bias=EPS, scale=1.0 / D),
         reads=[psbb], writes=[tmpb[2]])
    S.op("dve", lambda e: e.reciprocal(tmp[:, 2, :w], tmp[:, 2, :w]), reads=[tmpb[2]], writes=[tmpb[2]])
    yield
    for c in range(NCH):
        S.op("dve", lambda e, c=c: e.tensor_tensor(tmp[:, c % 2, :w], ht[:, c, :w], tmp[:, 2, :w], ALU.mult),
             reads=[htb, tmpb[2]], writes=[tmpb[c % 2]])
        S.op("act", lambda e, c=c: e.activation(xm[:, c, :w], tmp[:, c % 2, :w], AF.Identity,
                                                bias=g["modv"][:, s, 3 * n * NCH + c:3 * n * NCH + c + 1],
                                                scale=g["av"][:, s, n, c:c + 1]),
             reads=[tmpb[c % 2], G["modv"], G["av"]], writes=[xmb])
        yield


def phase_ffn2(P, layer, sidx, ffn_w1, ffn_w3, ffn_w2):
    S, nc, g = P.S, P.nc, P.g
    G = g["G"]
    hT = P.hT
    n = 0 if sidx == 0 else 2
    last_ctx_skip = (layer == 3 and sidx == 1)
    FS = 256
    CS = 256
    tiles = [(ti, t0, w) for ti, (t0, w) in enumerate(TILES) if not (ti == 0 and last_ctx_skip)]
    wbuf = P.wbuf[(layer, sidx)]
    kk_ = P.ffn_list.index((layer, sidx))
    if kk_ + 1 < len(P.ffn_list):
        cvt_ffn(P, P.ffn_list[kk_ + 1])
    with ExitStack() as es:
        ht = P.sb(es, "ht", [128, NCH, 512], F32)
        xm = [P.sb(es, f"xm{i}", [128, NCH, 512], BF16) for i in range(2)]
        hid = P.sb(es, "hid", [128, NFF, 512], BF16)
        tmp = P.sb(es, "tmp", [128, 2, 512], F32)
        ntmp = P.sb(es, "ntmp", [128, 3, 512], F32)
        hres = [P.sb(es, f"hres{i}", [128, 512], F32) for i in range(3)]
        NW = 3
        wsl = [P.sb(es, f"wsl{i}", [128, NFF * CS], BF16) for i in range(NW)]
        htb = S.buf("ht")
        xmb = [S.buf(f"xm{i}") for i in range(2)]
        tmpb = [S.buf(f"tmp{i}") for i in range(2)]
        ntmpb = [S.buf(f"ntmp{i}") for i in range(3)]
        hresb = [S.buf(f"hres{i}") for i in range(3)]
        hidb = [S.buf(f"hid{f}") for f in range(NFF)]
        wslb = [S.buf(f"wsl{i}") for i in range(NW)]
        pss = [P.ps(es, f"ps{i}", [128, 512], F32) for i in range(7)]
        pssb = [S.buf(f"ps{i}") for i in range(7)]

        def prologue(k):
            ti, t0, w = tiles[k]
            s = 1 if ti == 0 else 0
            def load():
                S.dma("sp", ht[:, :, :w], hT[:, t0:t0 + w].rearrange("(c p) t -> p c t", p=128), "ht_l", writes=[htb])
            return norm_steps(P, ht, htb, xm[k % 2], xmb[k % 2], w, s, n, pss[6], pssb[6], ntmp, ntmpb, load)

        for _ in prologue(0):
            pass
        wi = 0
        ri = 0
        for k, (ti, t0, w) in enumerate(tiles):
            s = 1 if ti == 0 else 0
            xk, xkb = xm[k % 2], xmb[k % 2]
            nxt = prologue(k + 1) if k + 1 < len(tiles) else iter(())
            for sl in range(DFF // FS):
                i = wi % NW
                wi += 1
                wt = wsl[i]
                w1v = wt[:, 0:NCH * FS].rearrange("p (k f) -> p k f", k=NCH)
                w3v = wt[:, NCH * FS:2 * NCH * FS].rearrange("p (k f) -> p k f", k=NCH)
                S.dma("sp", wt[:, 0:NCH * FS], P.wb1[layer, sidx, sl, :, :], f"wsl{i}", reads=[wbuf], writes=[wslb[i]])
                S.dma("sp", wt[:, NCH * FS:2 * NCH * FS], P.wb3[layer, sidx, sl, :, :], f"wsl{i}", reads=[wbuf], writes=[wslb[i]])
                for ff in range(FS // 128):
                    f = sl * (FS // 128) + ff
                    p1, p3 = pss[(2 * f) % 6], pss[(2 * f + 1) % 6]
                    p1b, p3b = pssb[(2 * f) % 6], pssb[(2 * f + 1) % 6]
                    for kk in range(NCH):
                        S.op("pe", lambda e, kk=kk, ff=ff, p1=p1, w1v=w1v: e.matmul(
                            p1[:, :w], w1v[:, kk, ff * 128:(ff + 1) * 128], xk[:, kk, :w],
                            start=(kk == 0), stop=(kk == NCH - 1)),
                            reads=[wslb[i], xkb], writes=[p1b], inc=(kk == NCH - 1))
                    for kk in range(NCH):
                        S.op("pe", lambda e, kk=kk, ff=ff, p3=p3, w3v=w3v: e.matmul(
                            p3[:, :w], w3v[:, kk, ff * 128:(ff + 1) * 128], xk[:, kk, :w],
                            start=(kk == 0), stop=(kk == NCH - 1)),
                            reads=[wslb[i], xkb], writes=[p3b], inc=(kk == NCH - 1))
                    tb = tmpb[f % 2]
                    S.op("act", lambda e, p1=p1, f=f: e.activation(tmp[:, f % 2, :w], p1[:, :w], AF.Silu),
                         reads=[p1b], writes=[tb])
                    S.op("dve", lambda e, p3=p3, f=f: e.tensor_tensor(hid[:, f, :w], tmp[:, f % 2, :w], p3[:, :w], ALU.mult),
                         reads=[tb, p3b], writes=[hidb[f]])
                    next(nxt, None)
            for _ in nxt:
                pass
            for sl in range(D // CS):
                i = wi % NW
                wi += 1
                wt = wsl[i]
                w2v = wt[:, :].rearrange("p (f c) -> p f c", f=NFF)
                S.dma("sp", wt[:, :], P.wb2[layer, sidx, sl, :, :], f"wsl{i}", reads=[wbuf], writes=[wslb[i]])
                for cc in range(CS // 128):
                    c = sl * (CS // 128) + cc
                    r = ri % 3
                    ri += 1
                    S.dma("sp", hres[r][:, :w], hT[c * 128:(c + 1) * 128, t0:t0 + w], f"hr{r}l", writes=[hresb[r]])
                    pp, ppb = pss[c % 6], pssb[c % 6]
                    for f in range(NFF):
                        S.op("pe", lambda e, f=f, cc=cc, pp=pp, w2v=w2v: e.matmul(
                            pp[:, :w], w2v[:, f, cc * 128:(cc + 1) * 128], hid[:, f, :w],
                            start=(f == 0), stop=(f == NFF - 1)),
                            reads=[wslb[i], hidb[f]], writes=[ppb], inc=(f == NFF - 1))
                    S.op("dve", lambda e, c=c, pp=pp, r=r: e.scalar_tensor_tensor(
                        hres[r][:, :w], pp[:, :w], g["gv"][:, s, n, c:c + 1], hres[r][:, :w], ALU.mult, ALU.add),
                        reads=[ppb, G["gv"], hresb[r]], writes=[hresb[r]])
                    S.dma("act", hT[c * 128:(c + 1) * 128, t0:t0 + w], hres[r][:, :w], f"hr{r}s", reads=[hresb[r]])
        S.barrier()


def phase_conv1(P, layer, conv_in_w):
    S, nc, g = P.S, P.nc, P.g
    G = g["G"]
    hT = P.hT
    e_ = layer // 2
    with ExitStack() as es:
        ht = P.sb(es, "ht", [128, NCH, 512], F32)
        xm2 = [P.sb(es, f"xm{i}", [128, NCH, 512], BF16) for i in range(2)]
        tmp = P.sb(es, "tmp", [128, 3, 512], F32)
        t2 = P.sb(es, "t2", [128, 2, 512], F32)
        wsl = [P.sb(es, f"cw{i}", [128, 5, NCH, 256], BF16) for i in range(2)]
        stg = [P.sb(es, f"stg{i}", [128, 3, 2, 512], F32) for i in range(2)]
        htb, tmpb = S.buf("ht"), [S.buf(f"tmp{i}") for i in range(3)]
        xm2b = [S.buf(f"xm{i}") for i in range(2)]

        def prologue(k):
            ti_, t0_, w_ = k, TILES[k][0], TILES[k][1]
            def load():
                S.dma("sp", ht[:, :, :w_], hT[:, t0_:t0_ + w_].rearrange("(c p) t -> p c t", p=128), "ht_l", writes=[htb])
            return norm_steps(P, ht, htb, xm2[k % 2], xm2b[k % 2], w_, 1 if k == 0 else 0, 1, pss[6], pssb[6], tmp, tmpb, load)

        t2b = [S.buf(f"t2{i}") for i in range(2)]
        wslb = [S.buf(f"cw{i}") for i in range(2)]
        stgb = [S.buf(f"stg{i}") for i in range(2)]
        pss = [P.ps(es, f"ps{i}", [128, 512], F32) for i in range(7)]
        pssb = [S.buf(f"ps{i}") for i in range(7)]
        wi = 0
        pi = 0
        for ti, (t0, w) in enumerate(TILES):
            s = 1 if ti == 0 else 0
            if ti == 0:
                for _ in prologue(0):
                    pass
            xm, xmb = xm2[ti % 2], xm2b[ti % 2]
            nxt = prologue(ti + 1) if ti + 1 < len(TILES) else iter(())
            for sl in range(4):
                i = wi % 2
                wi += 1
                for q in range(5):
                    c0 = q * 1024 + sl * 256
                    S.dma("pool", wsl[i][:, q, :, :],
                          conv_in_w[e_, :, c0:c0 + 256].rearrange("(k p) f -> p k f", p=128),
                          f"cw{i}", writes=[wslb[i]])
                for jj in range(2):
                    pq = []
                    for q in range(5):
                        pp, ppb = pss[pi % 6], pssb[pi % 6]
                        pi += 1
                        pq.append((pp, ppb))
                        for k in range(NCH):
                            S.op("pe", lambda e, k=k, q=q, jj=jj, pp=pp, i=i: e.matmul(
                                pp[:, :w], wsl[i][:, q, k, jj * 128:(jj + 1) * 128], xm[:, k, :w],
                                start=(k == 0), stop=(k == NCH - 1)),
                                reads=[wslb[i], xmb], writes=[ppb], inc=(k == NCH - 1))
                        next(nxt, None)
                    S.op("act", lambda e, pp=pq[1][0]: e.activation(t2[:, 0, :w], pp[:, :w], AF.Sigmoid),
                         reads=[pq[1][1]], writes=[t2b[0]])
                    S.op("dve", lambda e, pp=pq[0][0], i=i, jj=jj: e.tensor_tensor(
                        stg[i][:, 0, jj, :w], t2[:, 0, :w], pp[:, :w], ALU.mult),
                        reads=[t2b[0], pq[0][1]], writes=[stgb[i]])
                    S.op("act", lambda e, pp=pq[3][0]: e.copy(t2[:, 1, :w], pp[:, :w]),
                         reads=[pq[3][1]], writes=[t2b[1]])
                    S.op("dve", lambda e, pp=pq[4][0], i=i, jj=jj: e.tensor_tensor(
                        stg[i][:, 1, jj, :w], t2[:, 1, :w], pp[:, :w], ALU.mult),
                        reads=[t2b[1], pq[4][1]], writes=[stgb[i]])
                    S.op("act", lambda e, pp=pq[2][0], i=i, jj=jj: e.copy(stg[i][:, 2, jj, :w], pp[:, :w]),
                         reads=[pq[2][1]], writes=[stgb[i]])
                for q, dst in enumerate((P.apre, P.cvs, P.bgs)):
                    S.dma("act", dst[sl * 256:(sl + 1) * 256, t0:t0 + w].rearrange("(j p) t -> p j t", p=128),
                          stg[i][:, q, :, :w], f"stg{i}", reads=[stgb[i]])
            for _ in nxt:
                pass
        S.barrier()


def phase_conv3(P, layer, conv_out_w, convp):
    S, nc, g = P.S, P.nc, P.g
    G = g["G"]
    hT = P.hT
    e_ = layer // 2
    HA = 15
    with ExitStack() as es:
        hres = [P.sb(es, f"hres{i}", [128, 512], F32) for i in range(3)]
        hresb = [S.buf(f"hres{i}") for i in range(3)]
        dg = P.sb(es, "dg", [128, 8, 31, 128], BF16)
        idn = P.sb(es, "idn", [128, 128], F32)
        ap_bf = P.sb(es, "ap_bf", [128, 8, 512 + 2 * HA], BF16)
        dgb, idnb, apbfb = S.buf("dg"), S.buf("idn"), S.buf("ap_bf")
        ap_t = P.sb(es, "ap_t", [128, 8, 512 + 2 * HA], F32)
        cv_t = P.sb(es, "cv_t", [128, 8, 512 + 2], F32)
        bg_t = P.sb(es, "bg_t", [128, 8, 512], F32)
        acc = P.sb(es, "acc", [128, 8, 512], F32)
        amat = P.sb(es, "amat", [128, NCH, 512], BF16)
        tmp = P.sb(es, "tmp", [128, 6, 512], F32)
        cp = P.sb(es, "cp", [128, 8, 37], F32)
        wsl = [P.sb(es, f"ow{i}", [128, NCH, 256], BF16) for i in range(2)]
        htb, apb, cvb, bgb, accb, amb = (S.buf(x) for x in ("ht", "ap_t", "cv_t", "bg_t", "acc", "amat"))
        tmpb = [S.buf(f"tmp{i}") for i in range(6)]
        cpb = S.buf("cp")
        wslb = [S.buf(f"ow{i}") for i in range(2)]
        pss = [P.ps(es, f"ps{i}", [128, 512], F32) for i in range(6)]
        pssb = [S.buf(f"ps{i}") for i in range(6)]
        S.dma("sp", cp[:], convp[:, e_, :, :], "cp", writes=[cpb])
        S.dma("sp", idn[:], P.consts[:, 0:128], "cp", writes=[idnb])
        for j in range(8):
            for k in range(31):
                S.op("dve", lambda e, j=j, k=k: e.tensor_scalar(dg[:, j, k, :], idn[:], cp[:, j, k:k + 1], None, ALU.mult),
                     reads=[idnb, cpb], writes=[dgb])
        ri = 0
        wi = 0
        for ti, (t0, w) in enumerate(TILES):
            s = 1 if ti == 0 else 0
            seg0, seg1 = (0, LCTX) if ti == 0 else (LCTX, T)
            for (tile_, tb, src, H, key) in ((ap_t, apb, P.apre, HA, "ap_l"), (cv_t, cvb, P.cvs, 1, "cv_l")):
                lo, hi = max(t0 - H, seg0), min(t0 + w + H, seg1)
                if lo > t0 - H:
                    S.op("pool", lambda e, tile_=tile_, H=H: e.memset(tile_[:, :, 0:H], 0.0), writes=[tb])
                if hi < t0 + w + H:
                    S.op("pool", lambda e, tile_=tile_, H=H, w=w: e.memset(tile_[:, :, w + H:w + 2 * H], 0.0), writes=[tb])
                S.dma("sp", tile_[:, :, lo - (t0 - H):hi - (t0 - H)],
                      src[:, lo:hi].rearrange("(j p) t -> p j t", p=128), key, writes=[tb])
            S.dma("sp", bg_t[:, :, :w], P.bgs[:, t0:t0 + w].rearrange("(j p) t -> p j t", p=128), "bg_l", writes=[bgb])
            S.op("pool", lambda e, w=w: e.tensor_copy(ap_bf[:, :, :w + 2 * HA], ap_t[:, :, :w + 2 * HA]),
                 reads=[apb], writes=[apbfb])
            for j in range(8):
                pp, ppb = pss[j % 4], pssb[j % 4]
                for k in range(31):
                    S.op("pe", lambda e, j=j, k=k, pp=pp: e.matmul(pp[:, :w], dg[:, j, k, :], ap_bf[:, j, k:k + w],
                                                                  start=(k == 0), stop=(k == 30)),
                         reads=[dgb, apbfb], writes=[ppb], inc=(k == 30))
                S.op("act", lambda e, j=j, pp=pp: e.activation(acc[:, j, :w], pp[:, :w], AF.Identity, bias=cp[:, j, 31:32]),
                     reads=[ppb, cpb], writes=[accb])
            for j in range(8):
                S.op("pe", lambda e, j=j: e.matmul(pss[4][:, :w], g["ones_f"][:], acc[:, j, :w],
                                                   start=(j == 0), stop=(j == 7)),
                     reads=[accb, G["ones"]], writes=[pssb[4]], inc=(j == 7))
            for j in range(8):
                S.op("act", lambda e, j=j: e.activation(tmp[:, j % 2, :w], acc[:, j, :w], AF.Square),
                     reads=[accb], writes=[tmpb[j % 2]])
                S.op("pe", lambda e, j=j: e.matmul(pss[5][:, :w], g["ones_f"][:], tmp[:, j % 2, :w],
                                                   start=(j == 0), stop=(j == 7)),
                     reads=[tmpb[j % 2], G["ones"]], writes=[pssb[5]], inc=True)
            S.op("act", lambda e: e.mul(tmp[:, 2, :w], pss[4][:, :w], 1.0 / 1024), reads=[pssb[4]], writes=[tmpb[2]])
            S.op("dve", lambda e: e.tensor_tensor(tmp[:, 3, :w], tmp[:, 2, :w], tmp[:, 2, :w], ALU.mult),
                 reads=[tmpb[2]], writes=[tmpb[3]])
            S.op("dve", lambda e: e.scalar_tensor_tensor(tmp[:, 3, :w], pss[5][:, :w], 1.0 / 1024, tmp[:, 3, :w],
                                                         ALU.mult, ALU.subtract),
                 reads=[pssb[5], tmpb[3]], writes=[tmpb[3]])
            S.op("act", lambda e: e.activation(tmp[:, 3, :w], tmp[:, 3, :w], AF.Sqrt, bias=EPS, scale=1.0),
                 reads=[tmpb[3]], writes=[tmpb[3]])
            S.op("dve", lambda e: e.reciprocal(tmp[:, 3, :w], tmp[:, 3, :w]), reads=[tmpb[3]], writes=[tmpb[3]])
            for j in range(8):
                tb = tmpb[4 + j % 2]
                S.op("dve", lambda e, j=j: e.tensor_tensor(tmp[:, 4 + j % 2, :w], acc[:, j, :w], tmp[:, 2, :w], ALU.subtract),
                     reads=[accb, tmpb[2]], writes=[tb])
                S.op("dve", lambda e, j=j: e.tensor_tensor(tmp[:, 4 + j % 2, :w], tmp[:, 4 + j % 2, :w], tmp[:, 3, :w], ALU.mult),
                     reads=[tb, tmpb[3]], writes=[tb])
                S.op("act", lambda e, j=j: e.activation(amat[:, j, :w], tmp[:, 4 + j % 2, :w], AF.Silu,
                                                        bias=cp[:, j, 33:34], scale=cp[:, j, 32:33]),
                     reads=[tb, cpb], writes=[amb])
            for j in range(8):
                tb = tmpb[j % 2]
                S.op("dve", lambda e, j=j: e.tensor_scalar(tmp[:, j % 2, :w], cv_t[:, j, 0:w], cp[:, j, 34:35], None, ALU.mult),
                     reads=[cvb, cpb], writes=[tb])
                for k in (1, 2):
                    S.op("dve", lambda e, j=j, k=k: e.scalar_tensor_tensor(
                        tmp[:, j % 2, :w], cv_t[:, j, k:k + w], cp[:, j, 34 + k:35 + k], tmp[:, j % 2, :w], ALU.mult, ALU.add),
                        reads=[cvb, cpb, tb], writes=[tb])
                S.op("dve", lambda e, j=j: e.tensor_tensor(amat[:, 8 + j, :w], tmp[:, j % 2, :w], bg_t[:, j, :w], ALU.mult),
                     reads=[tb, bgb], writes=[amb])
            for sl in range(D // 256):
                i = wi % 2
                wi += 1
                S.dma("pool", wsl[i][:], conv_out_w[e_, :, sl * 256:(sl + 1) * 256].rearrange("(k p) f -> p k f", p=128),
                      f"ow{i}", writes=[wslb[i]])
                for cc in range(2):
                    c = sl * 2 + cc
                    pp, ppb = pss[c % 4], pssb[c % 4]
                    for k in range(NCH):
                        S.op("pe", lambda e, k=k, cc=cc, pp=pp, i=i: e.matmul(
                            pp[:, :w], wsl[i][:, k, cc * 128:(cc + 1) * 128], amat[:, k, :w],
                            start=(k == 0), stop=(k == NCH - 1)),
                            reads=[wslb[i], amb], writes=[ppb], inc=(k == NCH - 1))
                    r = ri % 3
                    ri += 1
                    S.dma("sp", hres[r][:, :w], hT[c * 128:(c + 1) * 128, t0:t0 + w], f"hr{r}l", writes=[hresb[r]])
                    S.op("dve", lambda e, c=c, pp=pp, r=r: e.scalar_tensor_tensor(
                        hres[r][:, :w], pp[:, :w], g["gv"][:, s, 1, c:c + 1], hres[r][:, :w], ALU.mult, ALU.add),
                        reads=[ppb, G["gv"], hresb[r]], writes=[hresb[r]])
                    S.dma("act", hT[c * 128:(c + 1) * 128, t0:t0 + w], hres[r][:, :w], f"hr{r}s", reads=[hresb[r]])
        S.barrier()


def phase_ssm1(P, layer, ssm_in_w):
    S, nc, g = P.S, P.nc, P.g
    G = g["G"]
    hT = P.hT
    o_ = layer // 2
    NCOL = 12416
    with ExitStack() as es:
        ht = P.sb(es, "ht", [128, NCH, 512], F32)
        xm2 = [P.sb(es, f"xm{i}", [128, NCH, 512], BF16) for i in range(2)]
        tmp = P.sb(es, "tmp", [128, 3, 512], F32)
        wsl = [P.sb(es, f"iw{i}", [128, NCH, 256], BF16) for i in range(3)]
        stg = [P.sb(es, f"stg{i}", [128, 2, 512], F32) for i in range(3)]
        htb, tmpb = S.buf("ht"), [S.buf(f"tmp{i}") for i in range(3)]
        xm2b = [S.buf(f"xm{i}") for i in range(2)]

        def prologue(k):
            ti_, t0_, w_ = k, TILES[k][0], TILES[k][1]
            def load():
                S.dma("sp", ht[:, :, :w_], hT[:, t0_:t0_ + w_].rearrange("(c p) t -> p c t", p=128), "ht_l", writes=[htb])
            return norm_steps(P, ht, htb, xm2[k % 2], xm2b[k % 2], w_, 1 if k == 0 else 0, 1, pss[6], pssb[6], tmp, tmpb, load)

        wslb = [S.buf(f"iw{i}") for i in range(3)]
        stgb = [S.buf(f"stg{i}") for i in range(3)]
        pss = [P.ps(es, f"ps{i}", [128, 512], F32) for i in range(7)]
        pssb = [S.buf(f"ps{i}") for i in range(7)]
        wi = 0
        pi = 0
        for ti, (t0, w) in enumerate(TILES):
            s = 1 if ti == 0 else 0
            if ti == 0:
                for _ in prologue(0):
                    pass
            xm, xmb = xm2[ti % 2], xm2b[ti % 2]
            nxt = prologue(ti + 1) if ti + 1 < len(TILES) else iter(())
            for sl in range((NCOL + 255) // 256):
                c0 = sl * 256
                cw = min(256, NCOL - c0)
                i = wi % 3
                wi += 1
                S.dma("sp", wsl[i][:].rearrange("p k f -> p (k f)"), P.wbi[o_, sl, :, :], f"iw{i}",
                      reads=[P.wbuf_ssm[o_]], writes=[wslb[i]])
                for jj in range(cw // 128):
                    pp, ppb = pss[pi % 6], pssb[pi % 6]
                    pi += 1
                    for k in range(NCH):
                        S.op("pe", lambda e, k=k, jj=jj, pp=pp, i=i: e.matmul(
                            pp[:, :w], wsl[i][:, k, jj * 128:(jj + 1) * 128], xm[:, k, :w],
                            start=(k == 0), stop=(k == NCH - 1)),
                            reads=[wslb[i], xmb], writes=[ppb], inc=(k == NCH - 1))
                    if c0 < 4096:
                        S.op("act", lambda e, pp=pp, i=i, jj=jj: e.activation(stg[i][:, jj, :w], pp[:, :w], AF.Silu),
                             reads=[ppb], writes=[stgb[i]])
                    elif pi % 2 == 0:
                        S.op("act", lambda e, pp=pp, i=i, jj=jj: e.copy(stg[i][:, jj, :w], pp[:, :w]),
                             reads=[ppb], writes=[stgb[i]])
                    else:
                        S.op("dve", lambda e, pp=pp, i=i, jj=jj: e.tensor_copy(stg[i][:, jj, :w], pp[:, :w]),
                             reads=[ppb], writes=[stgb[i]])
                    next(nxt, None)
                nj = cw // 128
                S.dma("act", P.U[c0:c0 + cw, t0:t0 + w].rearrange("(j p) t -> p j t", p=128),
                      stg[i][:, :nj, :w], f"stg{i}", reads=[stgb[i]])
            for _ in nxt:
                pass
        S.barrier()


def phase_ssm2(P, layer, scw, hp):
    S, nc, g = P.S, P.nc, P.g
    o_ = layer // 2
    colmaj = (o_ % 2 == 1)
    SEGS = ((0, LCTX), (LCTX, T))
    with ExitStack() as es:
        raw = [P.sb(es, f"raw{i}", [128, T], F32) for i in range(2)]
        prm = [P.sb(es, f"prm{i}", [128, T], F32) for i in range(2)] if colmaj else None
        acc = [P.sb(es, f"acc{i}", [128, T], F32) for i in range(2)]
        outb = [P.sb(es, f"outb{i}", [128, T], F32) for i in range(2)]
        cw_s = P.sb(es, "cw_s", [128, 2, 48, 5], F32)
        hp_s = P.sb(es, "hp_s", [64, 2, 2], F32)
        onesT = P.sb(es, "onesT", [64, T], F32)
        sbf = [P.sb(es, f"sbf{i}", [128, T], BF16) for i in range(2)]
        dg4 = [P.sb(es, f"dg4{i}", [128, 4, 128], BF16) for i in range(2)]
        idn = P.sb(es, "idn", [128, 128], F32)
        sbfb = [S.buf(f"sbf{i}") for i in range(2)]
        dg4b = [S.buf(f"dg4{i}") for i in range(2)]
        idnb = S.buf("idn")
        pss = [P.ps(es, f"ps{i}", [128, 512], F32) for i in range(6)]
        pssb = [S.buf(f"ps{i}") for i in range(6)]
        S.dma("sp", idn[:], P.consts[:, 0:128], "cp", writes=[idnb])
        cvi = [0, 0]
        rawb = [S.buf(f"raw{i}") for i in range(2)]
        prmb = [S.buf(f"prm{i}") for i in range(2)]
        accb = [S.buf(f"acc{i}") for i in range(2)]
        outbb = [S.buf(f"outb{i}") for i in range(2)]
        cwb, hpb, onesb = S.buf("cw_s"), S.buf("hp_s"), S.buf("onesT")
        S.dma("sp", cw_s[:], scw[:, o_, :, :, :], "cp", writes=[cwb])
        S.dma("sp", hp_s[:], hp[:, o_, :, :], "cp", writes=[hpb])
        S.op("pool", lambda e: e.memset(onesT[:], 1.0), writes=[onesb])
        S.op("act", lambda e: e.activation(hp_s[:, :, 1], hp_s[:, :, 1], AF.Exp), reads=[hpb], writes=[hpb])
        S.op("dve", lambda e: e.tensor_scalar_mul(hp_s[:, :, 1], hp_s[:, :, 1], -1.0), reads=[hpb], writes=[hpb])
        cnt = [0]

        def load(row0, nrows=128):
            i = cnt[0] % 2
            cnt[0] += 1
            S.dma("sp", raw[i][:nrows, :], P.U[row0:row0 + nrows, :], f"raw{i}", writes=[rawb[i]])
            if not colmaj:
                return raw[i], rawb[i], i
            S.op("pool", lambda e: e.tensor_copy(prm[i][:nrows, 0:LCTX], raw[i][:nrows, 0:LCTX]),
                 reads=[rawb[i]], writes=[prmb[i]])
            S.op("pool", lambda e: e.tensor_copy(
                prm[i][:nrows, LCTX:T].rearrange("p (c r) -> p c r", c=64),
                raw[i][:nrows, LCTX:T].rearrange("p (r c) -> p c r", r=64)),
                reads=[rawb[i]], writes=[prmb[i]])
            return prm[i], prmb[i], i

        def cast(src, srcb, i):
            S.op("pool", lambda e: e.tensor_copy(sbf[i][:], src[:]), reads=[srcb], writes=[sbfb[i]])

        def conv(src, srcb, i, d, cc, dst_rows):
            q = cvi[0] % 2
            cvi[0] += 1
            for k in range(4):
                S.op("dve", lambda e, k=k, q=q: e.tensor_scalar(dg4[q][:, k, :], idn[:], cw_s[:, d, cc, k:k + 1], None, ALU.mult),
                     reads=[idnb, cwb], writes=[dg4b[q]])
            ob, obb = outb[i], outbb[i]
            xb = sbf[i]
            for (s0, s1) in SEGS:
                tw = min(512, s1 - s0)
                for t0 in range(s0, s1, tw):
                    pp, ppb = pss[cvi[1] % 6], pssb[cvi[1] % 6]
                    cvi[1] += 1
                    for j in range(4):
                        if d == 0:
                            lo = max(t0, s0 + j)
                            oc = slice(lo - t0, tw)
                            ic = slice(lo - j, t0 + tw - j)
                        else:
                            hi = min(t0 + tw, s1 - j)
                            oc = slice(0, hi - t0)
                            ic = slice(t0 + j, hi + j)
                        S.op("pe", lambda e, j=j, oc=oc, ic=ic, pp=pp, q=q: e.matmul(
                            pp[:, oc], dg4[q][:, 3 - j, :], xb[:, ic], start=(j == 0), stop=(j == 3)),
                            reads=[dg4b[q], sbfb[i]], writes=[ppb], inc=(j == 3))
                    S.op("act", lambda e, t0=t0, tw=tw, pp=pp: e.activation(
                        ob[:, t0:t0 + tw], pp[:, :tw], AF.Silu, bias=cw_s[:, d, cc, 4:5]),
                        reads=[ppb, cwb], writes=[obb])
            S.dma("act", dst_rows, ob[:], f"outb{i}", reads=[obb])

        for cc in range(32):
            src, srcb, i = load(4096 + cc * 128)
            cast(src, srcb, i)
            for d in range(2):
                k = (2 * cc + d) % 2
                a_save = (acc[i], accb[i], outb[i], outbb[i])
                acc[i], accb[i], outb[i], outbb[i] = acc[k], accb[k], outb[k], outbb[k]
                conv(src, srcb, i, d, cc, P.XS[d, cc * 128:(cc + 1) * 128, :])
                acc[i], accb[i], outb[i], outbb[i] = a_save
        for d in range(2):
            base = 8192 + d * 2112
            for cc in range(16):
                src, srcb, i = load(base + cc * 128)
                cast(src, srcb, i)
                conv(src, srcb, i, d, 32 + cc, P.BC[d, cc * 128:(cc + 1) * 128, :])
        for d in range(2):
            src, srcb, i = load(10240 + d * 2112, 64)
            a, ab = acc[i], accb[i]
            ob, obb = outb[i], outbb[i]
            x_, e_, r_ = a[:64, :], ob[:64, :], acc[1 - i][:64, :]
            rb = accb[1 - i]
            S.op("dve", lambda e: e.tensor_scalar(x_, src[:64, :], hp_s[:, d, 0:1], None, ALU.add),
                 reads=[srcb, hpb], writes=[ab])
            S.op("act", lambda e: e.activation(e_, x_, AF.Abs), reads=[ab], writes=[obb])
            S.op("act", lambda e: e.activation(e_, e_, AF.Exp, scale=-1.0), reads=[obb], writes=[obb])
            S.op("act", lambda e: e.activation(e_, e_, AF.Ln, bias=1.0), reads=[obb], writes=[obb])
            S.op("dve", lambda e: e.tensor_scalar_max(r_, x_, 0.0), reads=[ab], writes=[rb])
            S.op("dve", lambda e: e.tensor_tensor(x_, r_, e_, ALU.add), reads=[rb, obb], writes=[ab])
            S.dma("act", P.DTS[d, 0, :, :], x_, f"acc{i}", reads=[ab])
            S.op("dve", lambda e: e.tensor_scalar(r_, x_, hp_s[:, d, 1:2], None, ALU.mult), reads=[ab, hpb], writes=[rb])
            S.op("dve", lambda e: e.tensor_tensor_scan(e_, onesT[:, :], r_, 0.0, ALU.mult, ALU.add),
                 reads=[rb, onesb], writes=[obb])
            ev = e_.rearrange("p (c j) -> p c j", j=128)
            rv = r_.rearrange("p (c j) -> p c j", j=128)
            NC_ = T // 128
            pl = raw[i][:64, :]
            plv = pl.rearrange("p (c j) -> p c j", j=128)
            plb = rawb[i]
            S.op("dve", lambda e: e.tensor_copy(plv[:, 0:1, :], ev[:, 0:1, :]), reads=[obb, srcb], writes=[plb])
            S.op("dve", lambda e: e.tensor_tensor(plv[:, 1:NC_, :], ev[:, 1:NC_, :],
                                                  ev[:, 0:NC_ - 1, 127:128].to_broadcast([64, NC_ - 1, 128]), ALU.subtract),
                 reads=[obb], writes=[plb])
            if d == 1:
                S.op("dve", lambda e: e.tensor_tensor(plv[:, :, :], plv[:, :, 127:128].to_broadcast([64, NC_, 128]),
                                                      plv[:, :, :], ALU.subtract),
                     reads=[plb], writes=[plb])
                S.op("dve", lambda e: e.tensor_tensor(pl, pl, r_, ALU.add), reads=[plb, rb], writes=[plb])
            S.dma("act", P.DTS[d, 1, :, :], pl, f"raw{i}s", reads=[plb])
        S.barrier()


def phase_ssm3(P, layer, dsk, selc):
    S, nc, g = P.S, P.nc, P.g
    G = g["G"]
    o_ = layer // 2
    NC_ = T // 128
    with ExitStack() as es:
        rbt = [P.sb(es, f"rbt{i}", [128, 32, 128], F32) for i in range(3)]
        cst = P.sb(es, "cst", [128, 4, 128], F32)
        cst2 = P.sb(es, "cst2", [128, 2, 128], F32)
        dsk_s = P.sb(es, "dsk_s", [128, 2, 32], F32)
        xs_t = [P.sb(es, f"xs_t{i}", [128, 32, 128], F32) for i in range(2)]
        bc_t = [P.sb(es, f"bc_t{i}", [128, 16, 128], F32) for i in range(2)]
        dd_t = [P.sb(es, f"dd_t{i}", [64, 2, 128], F32) for i in range(2)]
        bcb16 = P.sb(es, "bcb16", [128, 16, 128], BF16)
        tok = P.sb(es, "tok", [128, 6, 64], F32)
        xdt = P.sb(es, "xdt", [128, 4096], BF16)
        xdd = P.sb(es, "xdd", [128, 4096], BF16)
        btok = P.sb(es, "btok", [128, 1024], BF16)
        gm = P.sb(es, "gm", [128, 8, 128], F32)
        e4 = [P.sb(es, f"e4{i}", [128, 4, 128], F32) for i in range(4)]
        m4 = [P.sb(es, f"m4{i}", [128, 4, 128], BF16) for i in range(2)]
        c4 = [P.sb(es, f"c4{i}", [128, 4, 128], BF16) for i in range(2)]
        st = P.sb(es, "st", [128, 8, 512], F32)
        st16 = P.sb(es, "st16", [128, 8, 512], BF16)
        yo = [P.sb(es, "yo0", [128, 32, 128], F32)] * 2
        B_ = S.buf
        cstb, dskb = B_("cst"), B_("dsk_s")
        rbtb = [B_(f"rbt{i}") for i in range(3)]
        xsb = [B_(f"xs_t{i}") for i in range(2)]
        bcb = [B_(f"bc_t{i}") for i in range(2)]
        ddb = [B_(f"dd_t{i}") for i in range(2)]
        bc16b, tokb, xdtb, xddb, btokb, gmb = (B_(x) for x in ("bcb16", "tok", "xdt", "xdd", "btok", "gm"))
        e4b = [B_(f"e4{i}") for i in range(4)]
        m4b = [B_(f"m4{i}") for i in range(2)]
        c4b = [B_(f"c4{i}") for i in range(2)]
        stb = [B_(f"st{i}") for i in range(8)]
        st16b = [B_(f"st16{i}") for i in range(8)]
        yob = [B_("yo0")] * 2
        ptr = [P.ps(es, f"ptr{i}", [128, 512], F32) for i in range(2)]
        pY = [P.ps(es, f"pY{i}", [128, 512], F32) for i in range(2)]
        pG = P.ps(es, "pG", [128, 512], F32)
        pS = P.ps(es, "pS", [128, 512], F32)
        ptrb = [B_(f"ptr{i}") for i in range(2)]
        pYb = [B_(f"pY{i}") for i in range(2)]
        pGb, pSb = B_("pG"), B_("pS")

        S.dma("sp", cst[:].rearrange("p a t -> p (a t)"), P.consts[:, 0:512], "cp", writes=[cstb])
        S.dma("sp", cst2[:].rearrange("p a t -> p (a t)"), P.consts[:, 512:768], "cp", writes=[cstb])
        S.dma("sp", dsk_s[:], dsk[:, o_, :, :], "cp", writes=[dskb])
        ident = cst[:, 0, :]
        li = 0
        tri = 0
        ri = 0
        for d in range(2):
            order = ([0, 1] + list(range(2, NC_))) if d == 0 else ([1, 0] + list(range(NC_ - 1, 1, -1)))
            mask = cst[:, 1 + d, :]
            sel_last = cst2[:, d, :]
            for gi in range(8):
                S.op("pool", lambda e, gi=gi: e.memset(st[:, gi, :], 0.0), writes=[stb[gi]])
                S.op("pool", lambda e, gi=gi: e.memset(st16[:, gi, :], 0.0), writes=[st16b[gi]])
            for ci, c in enumerate(order):
                tk = slice(c * 128, (c + 1) * 128)
                i = li % 2
                li += 1
                S.dma("sp", xs_t[i][:], P.XS[d, :, tk].rearrange("(j p) t -> p j t", p=128), f"xs_t{i}", writes=[xsb[i]])
                S.dma("sp", bc_t[i][:], P.BC[d, :, tk].rearrange("(j p) t -> p j t", p=128), f"bc_t{i}", writes=[bcb[i]])
                S.dma("sp", dd_t[i][:], P.DTS[d, :, :, tk].rearrange("a h t -> h a t"), f"dd_t{i}", writes=[ddb[i]])
                rsel = []
                for half in range(2):
                    rj = ri % 3
                    ri += 1
                    S.dma("sp", rbt[rj][:], P.DTS[d, 1, half * 32:(half + 1) * 32, tk].partition_broadcast(128),
                          f"rbt{rj}", writes=[rbtb[rj]])
                    rsel.append(rj)
                S.op("act", lambda e, i=i: e.copy(bcb16[:], bc_t[i][:]), reads=[bcb[i]], writes=[bc16b])
                pt, ptb_ = ptr[tri % 2], ptrb[tri % 2]
                tri += 1
                S.op("pe", lambda e, i=i, pt=pt: e.transpose(pt[:, 0:64], dd_t[i][:, 0, :], ident[0:64, 0:64]),
                     reads=[ddb[i], cstb], writes=[ptb_], inc=False)
                S.op("pe", lambda e, i=i, pt=pt: e.transpose(pt[:, 64:128], dd_t[i][:, 1, :], ident[0:64, 0:64]),
                     reads=[ddb[i], cstb], writes=[ptb_])
                S.op("dve", lambda e, pt=pt: e.tensor_copy(tok[:, 0:2, :].rearrange("p a h -> p (a h)"), pt[:, 0:128]),
                     reads=[ptb_], writes=[tokb])
                S.op("pe", lambda e: e.matmul(pS[:, 0:64], sel_last, tok[:, 1, :], start=True, stop=True),
                     reads=[tokb, cstb], writes=[pSb])
                S.op("act", lambda e: e.activation(tok[:, 4, :], pS[:, 0:64], AF.Exp), reads=[pSb], writes=[tokb])
                S.op("dve", lambda e: e.tensor_tensor(tok[:, 2, :], pS[:, 0:64], tok[:, 1, :], ALU.subtract),
                     reads=[pSb, tokb], writes=[tokb])
                S.op("act", lambda e: e.activation(tok[:, 2, :], tok[:, 2, :], AF.Exp), reads=[tokb], writes=[tokb])
                S.op("dve", lambda e: e.tensor_scalar_mul(tok[:, 5, :], tok[:, 1, :], -1.0), reads=[tokb], writes=[tokb])
                S.op("dve", lambda e: e.tensor_tensor(tok[:, 3, :], tok[:, 2, :], tok[:, 0, :], ALU.mult),
                     reads=[tokb], writes=[tokb])
                for q in range(8):
                    pt, ptb_ = ptr[tri % 2], ptrb[tri % 2]
                    tri += 1
                    for jj in range(4):
                        S.op("pe", lambda e, i=i, pt=pt, q=q, jj=jj: e.transpose(
                            pt[:, jj * 128:(jj + 1) * 128], xs_t[i][:, 4 * q + jj, :], ident),
                            reads=[xsb[i], cstb], writes=[ptb_], inc=(jj == 3))
                    hs = slice(8 * q, 8 * q + 8)
                    S.op("dve", lambda e, pt=pt, q=q, hs=hs: e.tensor_tensor(
                        xdt[:, q * 512:(q + 1) * 512].rearrange("p (h c) -> p h c", h=8),
                        pt[:, :].rearrange("p (h c) -> p h c", h=8),
                        tok[:, 0, hs].unsqueeze(2).to_broadcast([128, 8, 64]), ALU.mult),
                        reads=[ptb_, tokb], writes=[xdtb])
                    S.op("dve", lambda e, pt=pt, q=q, hs=hs: e.tensor_tensor(
                        xdd[:, q * 512:(q + 1) * 512].rearrange("p (h c) -> p h c", h=8),
                        pt[:, :].rearrange("p (h c) -> p h c", h=8),
                        tok[:, 3, hs].unsqueeze(2).to_broadcast([128, 8, 64]), ALU.mult),
                        reads=[ptb_, tokb], writes=[xddb])
                for q in range(2):
                    pt, ptb_ = ptr[tri % 2], ptrb[tri % 2]
                    tri += 1
                    for jj in range(4):
                        S.op("pe", lambda e, i=i, pt=pt, q=q, jj=jj: e.transpose(
                            pt[:, jj * 128:(jj + 1) * 128], bc_t[i][:, 4 * q + jj, :], ident),
                            reads=[bcb[i], cstb], writes=[ptb_], inc=(jj == 3))
                    S.op("act", lambda e, pt=pt, q=q: e.copy(btok[:, q * 512:(q + 1) * 512], pt[:, :]),
                         reads=[ptb_], writes=[btokb])
                for q in range(2):
                    for jj in range(4):
                        gi = 4 * q + jj
                        S.op("pe", lambda e, gi=gi, jj=jj: e.matmul(
                            pG[:, jj * 128:(jj + 1) * 128], bcb16[:, gi, :], bcb16[:, 8 + gi, :], start=True, stop=True),
                            reads=[bc16b], writes=[pGb], inc=(jj == 3))
                    S.op("dve", lambda e, q=q: e.tensor_tensor(
                        gm[:, 4 * q:4 * q + 4, :], pG[:, :].rearrange("p (a t) -> p a t", a=4),
                        mask.unsqueeze(1).to_broadcast([128, 4, 128]), ALU.mult),
                        reads=[pGb, cstb], writes=[gmb])
                yi = li % 2
                for hb in range(16):
                    gi = hb // 2
                    b2 = hb % 2
                    rj = rsel[hb // 8]
                    hl0 = (4 * hb) % 32
                    for hh in range(4):
                        h = 4 * hb + hh
                        S.op("act", lambda e, hh=hh, h=h, rj=rj, hl0=hl0: e.activation(
                            e4[2 * b2][:, hh, :], rbt[rj][:, hl0 + hh, :], AF.Exp, bias=tok[:, 5, h:h + 1]),
                            reads=[rbtb[rj], tokb], writes=[e4b[2 * b2]])
                    S.op("dve", lambda e, b2=b2, gi=gi: e.scalar_tensor_tensor(
                        m4[b2][:], e4[2 * b2][:], 1.0, gm[:, gi:gi + 1, :].to_broadcast([128, 4, 128]), ALU.min, ALU.mult),
                        reads=[e4b[2 * b2], gmb], writes=[m4b[b2]])
                    S.op("act", lambda e, rj=rj, hl0=hl0: e.activation(e4[2 * b2 + 1][:], rbt[rj][:, hl0:hl0 + 4, :], AF.Exp),
                         reads=[rbtb[rj]], writes=[e4b[2 * b2 + 1]])
                    S.op("dve", lambda e, b2=b2, gi=gi: e.tensor_tensor(
                        c4[b2][:], e4[2 * b2 + 1][:], bcb16[:, 8 + gi:9 + gi, :].to_broadcast([128, 4, 128]), ALU.mult),
                        reads=[e4b[2 * b2 + 1], bc16b], writes=[c4b[b2]])
                    py, pyb = pY[gi % 2], pYb[gi % 2]
                    for hh in range(4):
                        h = 4 * hb + hh
                        cc = h // 2
                        po = (h % 2) * 64
                        ocol = slice((cc % 4) * 128, (cc % 4) * 128 + 128)
                        S.op("pe", lambda e, h=h, hh=hh, po=po, ocol=ocol, py=py, b2=b2: e.matmul(
                            py[po:po + 64, ocol], xdt[:, h * 64:(h + 1) * 64], m4[b2][:, hh, :], start=True, stop=False),
                            reads=[xdtb, m4b[b2]], writes=[pyb], inc=False)
                        S.op("pe", lambda e, h=h, hh=hh, po=po, ocol=ocol, py=py, b2=b2, gi=gi: e.matmul(
                            py[po:po + 64, ocol], st16[:, gi, (h % 8) * 64:(h % 8) * 64 + 64], c4[b2][:, hh, :],
                            start=False, stop=True),
                            reads=[st16b[gi], c4b[b2]], writes=[pyb], inc=(hh == 3))
                    if b2 == 1:
                        for jj in range(4):
                            cc = 4 * gi + jj
                            S.op("dve", lambda e, cc=cc, jj=jj, py=py, i=i, yi=yi: e.scalar_tensor_tensor(
                                yo[yi][:, cc, :], xs_t[i][:, cc, :], dsk_s[:, d, cc:cc + 1], py[:, jj * 128:(jj + 1) * 128],
                                ALU.mult, ALU.add),
                                reads=[xsb[i], dskb, pyb], writes=[yob[yi]])
                        S.op("pe", lambda e, gi=gi: e.matmul(pS[:, :], btok[:, gi * 128:(gi + 1) * 128],
                                                            xdd[:, gi * 512:(gi + 1) * 512], start=True, stop=True),
                             reads=[btokb, xddb], writes=[pSb])
                        S.op("dve", lambda e, gi=gi: e.tensor_tensor(
                            st[:, gi, :].rearrange("p (h c) -> p h c", h=8), st[:, gi, :].rearrange("p (h c) -> p h c", h=8),
                            tok[:, 4, 8 * gi:8 * gi + 8].unsqueeze(2).to_broadcast([128, 8, 64]), ALU.mult),
                            reads=[tokb, stb[gi]], writes=[stb[gi]])
                        S.op("dve", lambda e, gi=gi: e.tensor_tensor(st[:, gi, :], st[:, gi, :], pS[:, :], ALU.add),
                             reads=[pSb, stb[gi]], writes=[stb[gi]])
                        S.op("act", lambda e, gi=gi: e.copy(st16[:, gi, :], st[:, gi, :]), reads=[stb[gi]], writes=[st16b[gi]])
                S.dma("act", P.YS[d, :, tk].rearrange("(j p) t -> p j t", p=128), yo[yi][:], f"yo{yi}", reads=[yob[yi]])
        S.barrier()


def phase_ssm3b(P, layer):
    S, nc = P.S, P.nc
    o_ = layer // 2
    colmaj = (o_ % 2 == 1)
    with ExitStack() as es:
        ya = [P.sb(es, f"ya{i}", [128, T], F32) for i in range(2)]
        yb = [P.sb(es, f"yb{i}", [128, T], F32) for i in range(2)]
        yc = [P.sb(es, f"yc{i}", [128, T], F32) for i in range(2)]
        yab = [S.buf(f"ya{i}") for i in range(2)]
        ybb = [S.buf(f"yb{i}") for i in range(2)]
        ycb = [S.buf(f"yc{i}") for i in range(2)]
        for cc in range(32):
            i = cc % 2
            rows = slice(cc * 128, (cc + 1) * 128)
            S.dma("sp", ya[i][:], P.YS[0, rows, :], f"ya{i}", writes=[yab[i]])
            S.dma("sp", yb[i][:], P.YS[1, rows, :], f"yb{i}", writes=[ybb[i]])
            S.op("dve", lambda e, i=i: e.tensor_tensor(yc[i][:, 0:LCTX], ya[i][:, 0:LCTX], yb[i][:, 0:LCTX], ALU.add),
                 reads=[yab[i], ybb[i]], writes=[ycb[i]])
            if colmaj:
                S.op("dve", lambda e, i=i: e.tensor_tensor(
                    yc[i][:, LCTX:T].rearrange("p (r c) -> p c r", r=64),
                    ya[i][:, LCTX:T].rearrange("p (c r) -> p c r", c=64),
                    yb[i][:, LCTX:T].rearrange("p (c r) -> p c r", c=64), ALU.add),
                    reads=[yab[i], ybb[i]], writes=[ycb[i]])
            else:
                S.op("dve", lambda e, i=i: e.tensor_tensor(yc[i][:, LCTX:T], ya[i][:, LCTX:T], yb[i][:, LCTX:T], ALU.add),
                     reads=[yab[i], ybb[i]], writes=[ycb[i]])
            S.dma("act", P.YSUM[rows, :], yc[i][:], f"yc{i}", reads=[ycb[i]])
        S.barrier()


def phase_ssm4(P, layer, ssm_out_w, sng):
    S, nc, g = P.S, P.nc, P.g
    G = g["G"]
    hT = P.hT
    o_ = layer // 2
    last = (layer == 3)
    with ExitStack() as es:
        ht = P.sb(es, "ht", [128, NCH, 512], F32)
        ymat = P.sb(es, "ymat", [128, 32, 512], BF16)
        yt = [P.sb(es, f"yt{i}", [128, 4, 512], F32) for i in range(2)]
        zt = [P.sb(es, f"zt{i}", [128, 4, 512], F32) for i in range(2)]
        tmp = P.sb(es, "tmp", [128, 3, 512], F32)
        sng_s = P.sb(es, "sng_s", [128, 32], F32)
        wsl = [P.sb(es, f"sw{i}", [128, 32, 256], BF16) for i in range(2)]
        htb, ymb = S.buf("ht"), S.buf("ymat")
        ytb = [S.buf(f"yt{i}") for i in range(2)]
        ztb = [S.buf(f"zt{i}") for i in range(2)]
        tmpb = [S.buf(f"tmp{i}") for i in range(3)]
        sngb = S.buf("sng_s")
        wslb = [S.buf(f"sw{i}") for i in range(2)]
        pss = [P.ps(es, f"ps{i}", [128, 512], F32) for i in range(6)]
        pssb = [S.buf(f"ps{i}") for i in range(6)]
        S.dma("sp", sng_s[:], sng[:, o_, :], "cp", writes=[sngb])
        wi = 0
        gl = 0
        for ti, (t0, w) in enumerate(TILES):
            s = 1 if ti == 0 else 0
            if ti == 0 and last:
                continue
            S.dma("sp", ht[:, :, :w], hT[:, t0:t0 + w].rearrange("(c p) t -> p c t", p=128), "ht_l", writes=[htb])
            for gi in range(8):
                i = gl % 2
                gl += 1
                rows = slice(gi * 512, (gi + 1) * 512)
                S.dma("sp", yt[i][:, :, :w], P.YSUM[rows, t0:t0 + w].rearrange("(j p) t -> p j t", p=128), f"yt{i}", writes=[ytb[i]])
                S.dma("sp", zt[i][:, :, :w], P.U[rows, t0:t0 + w].rearrange("(j p) t -> p j t", p=128), f"zt{i}", writes=[ztb[i]])
                S.op("dve", lambda e, i=i: e.tensor_tensor(yt[i][:, :, :w], yt[i][:, :, :w], zt[i][:, :, :w], ALU.mult),
                     reads=[ytb[i], ztb[i]], writes=[ytb[i]])
                for j in range(4):
                    S.op("act", lambda e, i=i, j=j: e.activation(tmp[:, j % 2, :w], yt[i][:, j, :w], AF.Square),
                         reads=[ytb[i]], writes=[tmpb[j % 2]])
                    S.op("pe", lambda e, j=j: e.matmul(pss[4 + gi % 2][:, :w], g["ones_f"][:], tmp[:, j % 2, :w],
                                                       start=(j == 0), stop=(j == 3)),
                         reads=[tmpb[j % 2], G["ones"]], writes=[pssb[4 + gi % 2]])
                S.op("act", lambda e: e.activation(tmp[:, 2, :w], pss[4 + gi % 2][:, :w], AF.Sqrt, bias=EPS, scale=1.0 / 512),
                     reads=[pssb[4 + gi % 2]], writes=[tmpb[2]])
                S.op("dve", lambda e: e.reciprocal(tmp[:, 2, :w], tmp[:, 2, :w]), reads=[tmpb[2]], writes=[tmpb[2]])
                for j in range(4):
                    cc = 4 * gi + j
                    S.op("dve", lambda e, i=i, j=j, cc=cc: e.scalar_tensor_tensor(
                        ymat[:, cc, :w], yt[i][:, j, :w], sng_s[:, cc:cc + 1], tmp[:, 2, :w], ALU.mult, ALU.mult),
                        reads=[ytb[i], sngb, tmpb[2]], writes=[ymb])
            for sl in range(D // 256):
                i = wi % 2
                wi += 1
                S.dma("sp", wsl[i][:].rearrange("p k f -> p (k f)"), P.wbo[o_, sl, :, :], f"sw{i}",
                      reads=[P.wbuf_ssm[o_]], writes=[wslb[i]])
                for cc in range(2):
                    c = sl * 2 + cc
                    pp, ppb = pss[c % 4], pssb[c % 4]
                    for k in range(32):
                        S.op("pe", lambda e, k=k, cc=cc, pp=pp, i=i: e.matmul(
                            pp[:, :w], wsl[i][:, k, cc * 128:(cc + 1) * 128], ymat[:, k, :w],
                            start=(k == 0), stop=(k == 31)),
                            reads=[wslb[i], ymb], writes=[ppb], inc=(k == 31))
                    S.op("dve", lambda e, c=c, pp=pp: e.scalar_tensor_tensor(
                        ht[:, c, :w], pp[:, :w], g["gv"][:, s, 1, c:c + 1], ht[:, c, :w], ALU.mult, ALU.add),
                        reads=[ppb, G["gv"], htb], writes=[htb])
            S.dma("act", hT[:, t0:t0 + w].rearrange("(c p) t -> p c t", p=128), ht[:, :, :w], "ht_s", reads=[htb])
        S.barrier()


def phase_final(P, outT, do_norm):
    S, nc, g = P.S, P.nc, P.g
    G = g["G"]
    hT = P.hT
    with ExitStack() as es:
        ht = P.sb(es, "fht", [128, NCH, 512], F32)
        tmp = P.sb(es, "ftmp", [128, 3, 512], F32)
        htb, tmpb = S.buf("fht"), [S.buf(f"ftmp{i}") for i in range(3)]
        psb = P.ps(es, "fps", [128, 512], F32)
        psbb = S.buf("fps")
        for ti, (t0, w) in enumerate(TILES):
            if ti == 0:
                continue
            S.dma("sp", ht[:, :, :w], hT[:, t0:t0 + w].rearrange("(c p) t -> p c t", p=128), "ht_l", writes=[htb])
            if do_norm:
                for c in range(NCH):
                    S.op("act", lambda e, c=c: e.activation(tmp[:, c % 2, :w], ht[:, c, :w], AF.Square),
                         reads=[htb], writes=[tmpb[c % 2]])
                    S.op("pe", lambda e, c=c: e.matmul(psb[:, :w], g["ones_f"][:], tmp[:, c % 2, :w],
                                                       start=(c == 0), stop=(c == NCH - 1)),
                         reads=[tmpb[c % 2], G["ones"]], writes=[psbb])
                S.op("act", lambda e: e.activation(tmp[:, 2, :w], psb[:, :w], AF.Sqrt, bias=EPS, scale=1.0 / D),
                     reads=[psbb], writes=[tmpb[2]])
                S.op("dve", lambda e: e.reciprocal(tmp[:, 2, :w], tmp[:, 2, :w]),
                     reads=[tmpb[2]], writes=[tmpb[2]])
                for c in range(NCH):
                    S.op("dve", lambda e, c=c: e.scalar_tensor_tensor(
                        ht[:, c, :w], ht[:, c, :w], g["fgs"][:, c:c + 1], tmp[:, 2, :w], ALU.mult, ALU.mult),
                        reads=[htb, tmpb[2], G["fgs"]], writes=[htb])
            S.dma("act", outT[:, t0 - LCTX:t0 - LCTX + w].rearrange("(c p) t -> p c t", p=128), ht[:, :, :w], "ht_s",
                  reads=[htb])
        S.barrier()


FULL_STEPS = []
for _l in range(4):
    FULL_STEPS += [(_l, "mod"), (_l, "ffn0"), (_l, "mix"), (_l, "ffn1")]


def pcol(v, nchunks):
    return np.ascontiguousarray(np.asarray(v).reshape(nchunks, 128).T)


def make_inputs(inp, b):
    m = {}
    xT = np.empty((D, T), np.float32)
    xT[:, :LCTX] = inp["ctx"][b].T
    xT[:, LCTX:] = inp["x"][b].T
    m["xT"] = xT
    sc = np.empty((128, NCH, 2), np.float32)
    sc[:, :, 0] = pcol(inp["c"][b], NCH)
    sc[:, :, 1] = pcol(inp["c_ctx"], NCH)
    m["sc_in"] = sc
    m["mod_w"] = inp["mod_w"]
    m["mod_bT"] = np.ascontiguousarray(np.stack([pcol(inp["mod_b"][l], NMOD * NCH) for l in range(4)], axis=1))
    ng = np.empty((128, 4, 3, NCH), np.float32)
    for l in range(4):
        for n in range(3):
            ng[:, l, n, :] = pcol(inp["norm_g"][l, n], NCH)
    m["ngT"] = ng
    m["fgT"] = pcol(inp["final_g"], NCH)
    m["ffn_w1"] = inp["ffn_w1"]
    m["ffn_w3"] = inp["ffn_w3"]
    m["ffn_w2"] = inp["ffn_w2"]
    m["conv_in_w"] = inp["conv_in_w"]
    m["conv_out_w"] = inp["conv_out_w"]
    cvp = np.empty((128, 2, 8, 37), np.float32)
    for e in range(2):
        for k in range(31):
            cvp[:, e, :, k] = pcol(inp["conv_a_w"][e, k], 8)
        cvp[:, e, :, 31] = pcol(inp["conv_a_b"][e], 8)
        cvp[:, e, :, 32] = pcol(inp["conv_ln_g"][e], 8)
        cvp[:, e, :, 33] = pcol(inp["conv_ln_b"][e], 8)
        for k in range(3):
            cvp[:, e, :, 34 + k] = pcol(inp["conv_b_w"][e, k], 8)
    m["convp"] = cvp
    m["ssm_in_w"] = inp["ssm_in_w"]
    m["ssm_out_w"] = inp["ssm_out_w"]
    scw = np.empty((128, 2, 2, 48, 5), np.float32)
    hpv = np.empty((64, 2, 2, 2), np.float32)
    dsk = np.empty((128, 2, 2, 32), np.float32)
    sng = np.empty((128, 2, 32), np.float32)
    for o in range(2):
        sng[:, o, :] = pcol(inp["ssm_norm_g"][o], 32)
        for d in range(2):
            for k in range(4):
                scw[:, o, d, :, k] = pcol(inp["ssm_conv_w"][o, d, k], 48)
            scw[:, o, d, :, 4] = pcol(inp["ssm_conv_b"][o, d], 48)
            hpv[:, o, d, 0] = inp["ssm_dt_bias"][o, d]
            hpv[:, o, d, 1] = inp["ssm_a_log"][o, d]
            dsk[:, o, d, :] = pcol(np.repeat(inp["ssm_d"][o, d], 64), 32)
    m["scw"], m["hp"], m["dsk"], m["sng"] = scw, hpv, dsk, sng
    selc = np.zeros((64, 64, 128), np.float32)
    for h in range(64):
        selc[h, h, :] = 1.0
    m["selc"] = selc.reshape(64, 64 * 128)
    cst = np.zeros((128, 768), np.float32)
    ii = np.arange(128)
    cst[:, 0:128] = np.eye(128, dtype=np.float32)
    cst[:, 128:256] = (ii[:, None] <= ii[None, :])
    cst[:, 256:384] = (ii[:, None] >= ii[None, :])
    cst[127, 512:640] = 1.0
    cst[0, 640:768] = 1.0
    m["consts"] = cst
    return m


def kernel(**inputs):
    inp = {k: np.asarray(v) for k, v in inputs.items()}
    cfg = {"steps": FULL_STEPS}
    P = build(cfg)
    in_maps = [make_inputs(inp, b) for b in range(8)]
    res = run_bass_kernel_spmd(P.nc, in_maps, core_ids=list(range(8)))
    out = np.empty((8, LLAT, D), np.float32)
    for b in range(8):
        out[b] = res.results[b]["outT"].T
    return out
```
